# Optimizing a Trainium2 kernel written in Bass

```python
import math
import jax, jax.numpy as jnp
from jax import lax
import numpy as np

D_MODEL = 1024
BATCH = 4
SEQ = 8192
DEPTH = 4
DEC_BATCH = 8
DEC_SEQ = 64
PAST_LEN = 4096

CHUNK = 64
D_MIX = D_MODEL
D_SSM = D_MIX // 2
D_POOL = D_MIX - D_SSM
SSM_HEAD_DIM = 64
SSM_HEADS = D_SSM // SSM_HEAD_DIM
SSM_GROUPS = 2
SSM_HPG = SSM_HEADS // SSM_GROUPS
D_STATE = 128
CONV_W = 4
CONV_DIM = D_SSM + 2 * SSM_GROUPS * D_STATE
SSD_BLOCK = CHUNK
POOL_WINDOWS = (2, 4, 8, 16)
N_POOL_GROUPS = len(POOL_WINDOWS)
POOL_GROUP = D_POOL // N_POOL_GROUPS
POOL_HIST = max(POOL_WINDOWS) - 1
D_FF = ((8 * D_MODEL + 2) // 3 + 255) // 256 * 256
D_PLE = 256
IN_COLS = D_SSM + CONV_DIM + SSM_HEADS + D_POOL
EPS = 1e-6

kernel_name = "ssd_pool_hybrid_stream_step"


def _rmsnorm(x, g):
    xf = x.astype(jnp.float32)
    y = xf * lax.rsqrt(jnp.mean(xf * xf, axis=-1, keepdims=True) + EPS)
    return (y * g.astype(jnp.float32)).astype(x.dtype)


def _causal_dwconv(u, hist, w, b):
    ext = jnp.concatenate([hist.astype(u.dtype), u], axis=1)
    out = lax.conv_general_dilated(ext, w.astype(u.dtype)[:, None, :], window_strides=(1,), padding='VALID', dimension_numbers=('NWC', 'WIO', 'NWC'), feature_group_count=u.shape[-1])
    return out + b.astype(u.dtype), ext[:, ext.shape[1] - (CONV_W - 1):]


def _ssd_scan(x, dt, a, bm, cm, h0):
    f32 = jnp.float32
    nb, nl = x.shape[0], x.shape[1]
    q = SSD_BLOCK
    nc = -(-nl // q)
    pad = nc * q - nl
    x, dt, bm, cm = x.astype(f32), dt.astype(f32), bm.astype(f32), cm.astype(f32)
    if pad:
        padf = lambda t: jnp.pad(t, [(0, 0), (0, pad)] + [(0, 0)] * (t.ndim - 2))
        x, dt, bm, cm = padf(x), padf(dt), padf(bm), padf(cm)
    xc = x.reshape(nb, nc, q, SSM_GROUPS, SSM_HPG, SSM_HEAD_DIM)
    dtc = dt.reshape(nb, nc, q, SSM_GROUPS, SSM_HPG)
    bc = bm.reshape(nb, nc, q, SSM_GROUPS, D_STATE)
    cc = cm.reshape(nb, nc, q, SSM_GROUPS, D_STATE)
    acum = jnp.moveaxis(jnp.cumsum(dtc * a.reshape(SSM_GROUPS, SSM_HPG), axis=2), 2, -1)
    diff = acum[..., :, None] - acum[..., None, :]
    causal = jnp.tril(jnp.ones((q, q), dtype=bool))
    lmat = jnp.exp(jnp.where(causal, diff, -jnp.inf))
    xdt = xc * dtc[..., None]
    cb = jnp.einsum('bcign,bcjgn->bcgij', cc, bc)
    y_diag = jnp.einsum('bcgij,bcgrij,bcjgrp->bcigrp', cb, lmat, xdt)
    decay_end = jnp.exp(acum[..., -1:] - acum)
    chunk_states = jnp.einsum('bcjgn,bcgrj,bcjgrp->bcgrpn', bc, decay_end, xdt)
    chunk_decay = jnp.exp(acum[..., -1])

    def step(h, inp):
        s, d = inp
        return h * d[..., None, None] + s, h

    hg0 = h0.astype(f32).reshape(nb, SSM_GROUPS, SSM_HPG, SSM_HEAD_DIM, D_STATE)
    h_fin, h_in = lax.scan(step, hg0, (jnp.moveaxis(chunk_states, 1, 0), jnp.moveaxis(chunk_decay, 1, 0)))
    h_in = jnp.moveaxis(h_in, 0, 1)
    y_off = jnp.einsum('bcign,bcgrpn,bcgri->bcigrp', cc, h_in, jnp.exp(acum))
    y = (y_diag + y_off).reshape(nb, nc * q, SSM_HEADS, SSM_HEAD_DIM)[:, :nl]
    return y, h_fin.reshape(nb, SSM_HEADS, SSM_HEAD_DIM, D_STATE)


def _multiscale_pool(u, hist, pos0, pool_w, pool_b, pool_scale):
    nb, nl = u.shape[0], u.shape[1]
    ext = jnp.concatenate([hist.astype(u.dtype), u], axis=1)
    cs = jnp.pad(jnp.cumsum(ext.astype(jnp.float32), axis=1), ((0, 0), (1, 0), (0, 0)))
    end = cs[:, POOL_HIST + 1:]
    pos = pos0 + jnp.arange(nl)
    outs = []
    for gi, w in enumerate(POOL_WINDOWS):
        lo, hi = gi * POOL_GROUP, (gi + 1) * POOL_GROUP
        start = cs[:, POOL_HIST + 1 - w: POOL_HIST + 1 - w + nl, lo:hi]
        cnt = jnp.minimum(pos + 1, w).astype(jnp.float32)[None, :, None]
        outs.append((end[..., lo:hi] - start) / cnt)
    pooled = jnp.concatenate(outs, axis=-1) - u.astype(jnp.float32)
    pooled = pooled.reshape(nb, nl, N_POOL_GROUPS, POOL_GROUP)
    out = jnp.einsum('blgc,gcd->blgd', pooled, pool_w.astype(jnp.float32)) + pool_b.astype(jnp.float32)
    out = out.reshape(nb, nl, D_POOL) * pool_scale.astype(jnp.float32)
    return out, ext[:, ext.shape[1] - POOL_HIST:]


def _layer(h, p_i, conv_hist, ssm_h0, pool_hist, pos0, pre_mix_g, w_in, conv_w, conv_b, dt_bias, a_log, d_skip, ssm_norm_g, pool_w, pool_b, pool_scale, w_out, post_mix_g, pre_ffn_g, w_gate, w_up, w_down, post_ffn_g, w_ple_gate, w_ple_proj, ple_norm_g):
    f32 = jnp.float32
    nb, nl = h.shape[0], h.shape[1]
    hn = _rmsnorm(h, pre_mix_g)
    proj = hn @ w_in
    z, xbc, dt_raw, u_pool = jnp.split(proj, [D_SSM, D_SSM + CONV_DIM, D_SSM + CONV_DIM + SSM_HEADS], axis=-1)
    xbc, new_conv = _causal_dwconv(xbc, conv_hist, conv_w, conv_b)
    xbc = jax.nn.silu(xbc)
    xs, bm, cm = jnp.split(xbc, [D_SSM, D_SSM + SSM_GROUPS * D_STATE], axis=-1)
    xs = xs.reshape(nb, nl, SSM_HEADS, SSM_HEAD_DIM)
    bm = bm.reshape(nb, nl, SSM_GROUPS, D_STATE)
    cm = cm.reshape(nb, nl, SSM_GROUPS, D_STATE)
    dt = jax.nn.softplus(dt_raw.astype(f32) + dt_bias.astype(f32))
    a = -jnp.exp(a_log.astype(f32))
    y, new_ssm = _ssd_scan(xs, dt, a, bm, cm, ssm_h0)
    y = y + d_skip.astype(f32)[:, None] * xs.astype(f32)
    y = y.reshape(nb, nl, D_SSM) * jax.nn.silu(z.astype(f32))
    yg = y.reshape(nb, nl, SSM_GROUPS, D_SSM // SSM_GROUPS)
    yg = yg * lax.rsqrt(jnp.mean(yg * yg, axis=-1, keepdims=True) + EPS)
    y = yg.reshape(nb, nl, D_SSM) * ssm_norm_g.astype(f32)
    pool_out, new_pool = _multiscale_pool(u_pool, pool_hist, pos0, pool_w, pool_b, pool_scale)
    mix = jnp.concatenate([y.astype(h.dtype), pool_out.astype(h.dtype)], axis=-1) @ w_out
    h = h + _rmsnorm(mix, post_mix_g)
    hn = _rmsnorm(h, pre_ffn_g)
    f = (jax.nn.silu(hn @ w_gate) * (hn @ w_up)) @ w_down
    h = h + _rmsnorm(f, post_ffn_g)
    gate = jax.nn.sigmoid((h @ w_ple_gate).astype(f32))
    e = (p_i @ w_ple_proj).astype(f32) * gate
    h = h + _rmsnorm(e, ple_norm_g).astype(h.dtype)
    return h, new_conv, new_ssm, new_pool


def setup_inputs(seed: int = 0) -> dict:
    key = jax.random.key(seed)
    ks = jax.random.split(key, 40)
    f32 = jnp.float32

    def nrm(k, shape, scale):
        return jax.random.normal(k, shape, f32) * scale

    def gain(k, n):
        return 1.0 + 0.05 * jax.random.normal(k, (DEPTH, n), f32)

    dt0 = jnp.exp(jax.random.uniform(ks[10], (DEPTH, SSM_HEADS), f32) * (math.log(0.1) - math.log(0.001)) + math.log(0.001))
    dt_bias = dt0 + jnp.log(-jnp.expm1(-dt0))
    a_log = jnp.log(jax.random.uniform(ks[11], (DEPTH, SSM_HEADS), f32, 1.0, 16.0))
    return {
        'x_prompt': nrm(ks[0], (BATCH, SEQ, D_MODEL), 1.0),
        'x_sample': nrm(ks[1], (DEC_BATCH, DEC_SEQ, D_MODEL), 1.0),
        'state_ssm': nrm(ks[2], (DEPTH, DEC_BATCH, SSM_HEADS, SSM_HEAD_DIM, D_STATE), 0.5),
        'state_conv': nrm(ks[3], (DEPTH, DEC_BATCH, CONV_W - 1, CONV_DIM), 1.0),
        'state_pool': nrm(ks[4], (DEPTH, DEC_BATCH, POOL_HIST, D_POOL), 1.0),
        'p_prompt': nrm(ks[5], (DEPTH, BATCH, SEQ, D_PLE), 1.0),
        'p_sample': nrm(ks[6], (DEPTH, DEC_BATCH, DEC_SEQ, D_PLE), 1.0),
        'pre_mix_g': gain(ks[7], D_MODEL),
        'w_in': nrm(ks[8], (DEPTH, D_MODEL, IN_COLS), D_MODEL ** -0.5),
        'conv_w': nrm(ks[9], (DEPTH, CONV_W, CONV_DIM), CONV_W ** -0.5),
        'conv_b': nrm(ks[12], (DEPTH, CONV_DIM), 0.01),
        'dt_bias': dt_bias,
        'a_log': a_log,
        'd_skip': 1.0 + 0.05 * jax.random.normal(ks[13], (DEPTH, SSM_HEADS), f32),
        'ssm_norm_g': gain(ks[14], D_SSM),
        'pool_w': nrm(ks[15], (DEPTH, N_POOL_GROUPS, POOL_GROUP, POOL_GROUP), POOL_GROUP ** -0.5),
        'pool_b': nrm(ks[16], (DEPTH, N_POOL_GROUPS, POOL_GROUP), 0.01),
        'pool_scale': 1.0 + 0.05 * jax.random.normal(ks[17], (DEPTH, D_POOL), f32),
        'w_out': nrm(ks[18], (DEPTH, D_MIX, D_MODEL), D_MIX ** -0.5),
        'post_mix_g': gain(ks[19], D_MODEL),
        'pre_ffn_g': gain(ks[20], D_MODEL),
        'w_gate': nrm(ks[21], (DEPTH, D_MODEL, D_FF), D_MODEL ** -0.5),
        'w_up': nrm(ks[22], (DEPTH, D_MODEL, D_FF), D_MODEL ** -0.5),
        'w_down': nrm(ks[23], (DEPTH, D_FF, D_MODEL), D_FF ** -0.5),
        'post_ffn_g': gain(ks[24], D_MODEL),
        'w_ple_gate': nrm(ks[25], (DEPTH, D_MODEL, D_MODEL), D_MODEL ** -0.5),
        'w_ple_proj': nrm(ks[26], (DEPTH, D_PLE, D_MODEL), D_PLE ** -0.5),
        'ple_norm_g': gain(ks[27], D_MODEL),
    }


def reference(x_prompt, x_sample, state_ssm, state_conv, state_pool, p_prompt, p_sample, pre_mix_g, w_in, conv_w, conv_b, dt_bias, a_log, d_skip, ssm_norm_g, pool_w, pool_b, pool_scale, w_out, post_mix_g, pre_ffn_g, w_gate, w_up, w_down, post_ffn_g, w_ple_gate, w_ple_proj, ple_norm_g):
    nbp = x_prompt.shape[0]
    conv0 = jnp.zeros((nbp, CONV_W - 1, CONV_DIM), x_prompt.dtype)
    ssm0 = jnp.zeros((nbp, SSM_HEADS, SSM_HEAD_DIM, D_STATE), jnp.float32)
    pool0 = jnp.zeros((nbp, POOL_HIST, D_POOL), x_prompt.dtype)
    hp, hs = x_prompt, x_sample
    ssm_p, conv_p, pool_p, ssm_s, conv_s, pool_s = [], [], [], [], [], []
    for i in range(DEPTH):
        lw = (pre_mix_g[i], w_in[i], conv_w[i], conv_b[i], dt_bias[i], a_log[i], d_skip[i], ssm_norm_g[i], pool_w[i], pool_b[i], pool_scale[i], w_out[i], post_mix_g[i], pre_ffn_g[i], w_gate[i], w_up[i], w_down[i], post_ffn_g[i], w_ple_gate[i], w_ple_proj[i], ple_norm_g[i])
        hp, c, s, q = _layer(hp, p_prompt[i], conv0, ssm0, pool0, 0, *lw)
        conv_p.append(c)
        ssm_p.append(s)
        pool_p.append(q)
        hs, c, s, q = _layer(hs, p_sample[i], state_conv[i], state_ssm[i], state_pool[i], PAST_LEN, *lw)
        conv_s.append(c)
        ssm_s.append(s)
        pool_s.append(q)
    return (hp, hs, jnp.stack(ssm_p), jnp.stack(conv_p), jnp.stack(pool_p), jnp.stack(ssm_s), jnp.stack(conv_s), jnp.stack(pool_s))
```

```python
import numpy as np
from contextlib import ExitStack
import concourse.bass as bass
import concourse.mybir as mybir
from concourse.bass_utils import run_bass_kernel_spmd

F32 = mybir.dt.float32
BF16 = mybir.dt.bfloat16
AF = mybir.ActivationFunctionType
ALU = mybir.AluOpType

D_MODEL = 1024
DEPTH = 4
D_SSM = 512
D_FF = 2816
D_PLE = 256
IN_COLS = 2056
EPS = 1e-6
POOL_W = (2, 4, 8, 16)
TT = 512
NSLAB = 31
SLABW = 4096
PL = 112
NRING = 4

SEM_EPOCH = 30000
PAIRS = [[0, 1], [2, 3], [4, 5], [6, 7]]
LAG = 2


class _Op:
    __slots__ = ("eng", "fn", "dma", "deps", "sig", "signal", "need", "inc")

    def __init__(self, eng, fn, dma):
        self.eng = eng
        self.fn = fn
        self.dma = dma
        self.deps = None
        self.sig = None
        self.signal = False


class Prog:
    def __init__(self):
        self.ops = []
        self.lastw = {}
        self.readers = {}

    enabled = True

    def add(self, eng, fn, reads=(), writes=(), dma=None, inc=16):
        if not self.enabled:
            return None
        idx = len(self.ops)
        op = _Op(eng, fn, dma)
        op.inc = inc
        agent = ("dma", dma) if dma else eng
        raw = set()
        oth = set()
        lastw = self.lastw
        readers = self.readers
        for k in reads:
            w = lastw.get(k)
            if w is not None:
                raw.add(w)
        for k in writes:
            w = lastw.get(k)
            if w is not None:
                oth.add(w)
            rd = readers.get(k)
            if rd:
                oth.update(rd.values())
        for k in reads:
            rd = readers.get(k)
            if rd is None:
                readers[k] = {agent: idx}
            else:
                rd[agent] = idx
        for k in writes:
            lastw[k] = idx
            readers[k] = {}
        deps = set(raw)
        ops = self.ops
        for d in oth:
            if d in raw:
                continue
            od = ops[d]
            if eng == "pe" and od.eng == "pe" and dma is None and od.dma is None:
                continue
            deps.add(d)
        op.deps = deps
        ops.append(op)
        return idx

    def emit(self, nc, block, sem_pool):
        ops = self.ops
        for op in ops:
            for d in op.deps:
                ops[d].signal = True
        counters = {}
        for op in ops:
            need = {}
            for d in op.deps:
                k, v = ops[d].sig
                if k[0] == "dma":
                    v = counters[k]
                if need.get(k, 0) < v:
                    need[k] = v
            op.need = need
            if op.dma:
                key = ("dma", op.dma)
                c = counters.get(key, 0) + op.inc
                counters[key] = c
                op.sig = (key, c)
            elif op.signal:
                c = counters.get(op.eng, 0) + 1
                counters[op.eng] = c
                ep = (c - 1) // SEM_EPOCH
                op.sig = ((op.eng, ep), c - ep * SEM_EPOCH)
        semh = {}
        for op in ops:
            if op.sig is not None and op.sig[0] not in semh:
                semh[op.sig[0]] = sem_pool.pop()

        def make_stream(ename):
            def run(e):
                waited = {}
                for op in ops:
                    if op.eng != ename:
                        continue
                    pend = [(k, v) for k, v in op.need.items() if waited.get(k, 0) < v]
                    for k, v in pend:
                        waited[k] = v
                    emb = pend.pop() if (pend and op.inc != 1) else None
                    for k, v in pend:
                        e.wait_ge(semh[k], v)
                    ins = op.fn(e)
                    if emb is not None:
                        ins._wait_ge(semh[emb[0]], emb[1])
                    if op.sig is not None:
                        ins.then_inc(semh[op.sig[0]], op.inc if op.dma else 1)
                fin = {}
                for op in ops:
                    if op.eng == ename and op.dma:
                        k, v = op.sig
                        fin[k] = max(fin.get(k, 0), v)
                for k, v in fin.items():
                    if waited.get(k, 0) < v:
                        e.wait_ge(semh[k], v)
            return run

        block.tensor(make_stream("pe"))
        block.scalar(make_stream("act"))
        block.vector(make_stream("dve"))
        block.gpsimd(make_stream("pool"))
        block.sync(make_stream("sp"))
        return counters


def slab_shapes():
    sh = [None] * NSLAB
    sh[0] = (8, 512)
    sh[1] = (8, 512)
    sh[2] = (8, 512)
    sh[3] = (8, 8)
    sh[4] = (8, 512)
    sh[5] = (4, 128)
    sh[6] = (8, 512)
    sh[7] = (8, 512)
    for j in range(6):
        nc_ = 512 if j < 5 else 256
        sh[8 + 2 * j] = (8, nc_)
        sh[9 + 2 * j] = (8, nc_)
    for m in range(8):
        sh[20 + m] = (22, 128)
    sh[28] = (8, 512)
    sh[29] = (8, 512)
    sh[30] = (2, 1024)
    return sh


SLAB_SH = slab_shapes()


def build_program(n_ptiles, depth):
    NPT = (n_ptiles + LAG) * TT
    L = depth
    nc = bass.Bass("TRN2", target_bir_lowering=False)
    dram_in = lambda n, s, d=F32: nc.dram_tensor(n, s, d, kind="ExternalInput").ap()
    dram_out = lambda n, s, d=F32: nc.dram_tensor(n, s, d, kind="ExternalOutput").ap()
    xT = dram_in("xT", [1024, NPT])
    xsT = dram_in("xsT", [1024, 128])
    ppT = dram_in("ppT", [L, 256, NPT])
    psT = dram_in("psT", [L, 256, 128])
    hs0 = dram_in("hs0", [L, 2, 128, 512])
    cv0 = dram_in("cv0", [L, 2, 128, 8, 3])
    pl0 = dram_in("pl0", [L, 2, 128, 4, 15])
    wsl = dram_in("wsl", [L, NSLAB, 128, SLABW])
    prm_d = dram_in("prm", [128, L * PL + 8])
    cst_d = dram_in("cst", [128, 512 + 64 * (LAG + 1)])
    wbf = nc.dram_tensor("wbf", [L, NSLAB, 128, SLABW], BF16, kind="Internal").ap()
    sx_t = nc.dram_tensor("sx", [1024, 128], F32)
    gs_t = nc.dram_tensor("gs", [2048, 128], F32)
    px_t = [nc.dram_tensor(f"px{i}", [1024, TT], F32) for i in range(2)]
    gp_t = [nc.dram_tensor(f"gp{i}", [2048, TT], F32) for i in range(2)]
    yT = dram_out("yT", [1024, NPT])
    ysT = dram_out("ysT", [2, 1024, 128])
    ssm_p = dram_out("ssm_p", [2, L, 128, 512])
    conv_p = dram_out("conv_p", [2, L, 128, 8, 3])
    pool_p = dram_out("pool_p", [2, L, 128, 4, 15])
    ssm_s = dram_out("ssm_s", [2, L, 2, 128, 512])
    conv_s = dram_out("conv_s", [2, L, 2, 128, 8, 3])
    pool_s = dram_out("pool_s", [2, L, 2, 128, 4, 15])

    P = Prog()
    with ExitStack() as st:
        sb = lambda n, s, d=F32: st.enter_context(nc.sbuf_tensor(n, s, d))
        pst = lambda n, s, d=F32: st.enter_context(nc.psum_tensor(n, s, d))
        h = sb("h", [128, 8, TT])
        hn = sb("hn", [128, 8, TT], BF16)
        sqb = sb("sqb", [128, 2, TT], BF16)
        rs = sb("rs", [128, 2, TT])
        zs = sb("zs", [128, 4, TT])
        XW = 3 + TT
        xbc = sb("xbc", [128, 8, XW])
        xs = sb("xs", [128, 4, TT])
        BCb = sb("BCb", [128, 4, TT], BF16)
        UW = 15 + TT
        uex = sb("uex", [128, 4, UW])
        tA = sb("tA", [128, UW])
        tB = sb("tB", [128, UW])
        tC = sb("tC", [128, UW])
        tD = sb("tD", [128, UW])
        mix = sb("mix", [128, 8, TT])
        actb = sb("actb", [128, 22, TT], BF16)
        bufA = [sb(f"bufA{i}", [128, 8, 128]) for i in range(2)]
        LTb = [sb(f"LTb{i}", [128, 8, 128], BF16) for i in range(2)]
        eR1 = sb("eR", [128, 8, 128], BF16)
        eR = [eR1, eR1]
        Csb = [sb(f"Csb{i}", [128, 8, 128], BF16) for i in range(2)]
        xdtb = [sb(f"xdtb{i}", [128, 512], BF16) for i in range(2)]
        xdtdb = [sb(f"xdtdb{i}", [128, 512], BF16) for i in range(2)]
        BTb = [sb(f"BTb{i}", [128, 2, 128], BF16) for i in range(2)]
        sm = sb("sm", [128, 256])
        dD = sb("dD", [128, L, 4, 128])
        hT = sb("hT", [128, L, 2, 512])
        hTb = sb("hTb", [128, 2, 512], BF16)
        hcv = sb("hcv", [128, L, 2, 8, 3])
        hpl = sb("hpl", [128, L, 2, 4, 15])
        pTb = sb("pTb", [128, 2, TT], BF16)
        cst = sb("cst_sb", [128, 512 + 64 * (LAG + 1)])
        cstb = sb("cstb", [128, 384], BF16)
        prm = sb("prm_sb", [128, L * PL + 8])
        wdt = sb("wdt", [128, 2, 64], BF16)
        wpl = sb("wpl", [128, 2, 512], BF16)
        ring = [sb(f"ring{i}", [128, SLABW], BF16) for i in range(NRING)]
        PB = [pst(f"P{i}", [128, 512]) for i in range(7)]
        P7a = pst("P7a", [128, 512])
        P7b = PB[3][:, 0:128].bitcast(BF16)
        sems = [st.enter_context(nc.semaphore(f"s{i}")) for i in range(48)]
        block = st.enter_context(nc.Block())

        ident = cst[:, 0:128]
        tri = cst[:, 128:256]
        ones = cst[:, 256:384]
        negm = cst[:, 384:512]
        identb = cstb[:, 0:128]
        onesb = cstb[:, 256:384]

        def mm(out, lhsT, rhs, start, stop, r, w):
            P.add("pe", lambda e: e.matmul(out, lhsT, rhs, start=start, stop=stop), r, w)

        def tr(out, in_, idn, r, w):
            P.add("pe", lambda e: e.transpose(out, in_, idn), r, w)

        def act(out, in_, func, r, w, bias=None, scale=None):
            kw = {}
            if bias is not None:
                kw["bias"] = bias
            if scale is not None:
                kw["scale"] = scale
            P.add("act", lambda e: e.activation(out, in_, func, **kw), r, w)

        def tt(eng, out, a, b, op, r, w):
            P.add(eng, lambda e: e.tensor_tensor(out, a, b, op), r, w)

        def ts(eng, out, a, s1, s2, op0, op1, r, w):
            P.add(eng, lambda e: e.tensor_scalar(out, a, s1, s2, op0, op1), r, w)

        def ts1(eng, out, a, s1, op0, r, w):
            P.add(eng, lambda e: e.tensor_single_scalar(out, a, s1, op0), r, w)

        def stt(eng, out, in0, scalar, in1, op0, op1, r, w):
            P.add(eng, lambda e: e.scalar_tensor_tensor(out, in0, scalar, in1, op0, op1), r, w)

        def cp(eng, out, in_, r, w):
            if eng == "act":
                P.add("act", lambda e: e.activation(out, in_, AF.Copy), r, w)
            else:
                P.add(eng, lambda e: e.tensor_copy(out, in_), r, w)

        def dma(eng, out, in_, r, w, sem):
            P.add(eng, lambda e: e.dma_start(out=out, in_=in_), r, w, dma=sem)

        dma("pool", cst[:], cst_d, [], ["cst"], "su0")
        dma("pool", prm[:], prm_d, [], ["prm"], "su1")
        cp("dve", cstb[:], cst[:, 0:384], ["cst"], ["cstb"])
        for l in range(L):
            b0 = l * PL
            ts1("dve", prm[:, b0:b0 + 40], prm[:, b0:b0 + 40], 32.0, ALU.mult, ["prm"], ["prm"])
            ts1("dve", prm[:, b0 + 40:b0 + 44], prm[:, b0 + 40:b0 + 44], 16.0, ALU.mult, ["prm"], ["prm"])
            act(prm[:, b0 + 104:b0 + 112], prm[:, b0 + 104:b0 + 112], AF.Exp, ["prm"], ["prm"])
            ts1("dve", prm[:, b0 + 104:b0 + 112], prm[:, b0 + 104:b0 + 112], -1.0, ALU.mult, ["prm"], ["prm"])
        for l in range(L):
            for m in range(4):
                ts1("dve", dD[:, l, m, :], ident, prm[:, l * PL + 92 + m:l * PL + 93 + m], ALU.mult, ["cst", "prm"], ["dD"])
        sgrp = lambda s: 0 if s < 8 else (1 if s < 20 else 2)
        conv_done = set()

        def emit_conv(l, g):
            if l >= L or (l, g) in conv_done:
                return
            conv_done.add((l, g))
            s0, s1 = ((0, 8), (8, 20), (20, NSLAB))[g]
            src = wsl[l, s0:s1].rearrange("s p n -> (s p n)").rearrange("(a b) -> a b", a=16)
            dst = wbf[l, s0:s1].rearrange("s p n -> (s p n)").rearrange("(a b) -> a b", a=16)
            dma("pool", dst, src, [], [f"wbf{l}_{g}"], f"cv{l}_{g}")

        emit_conv(0, 0)

        import os
        _stop = os.environ.get("MK_STOP", "")

        _cnt = {}

        def stage(name):
            _cnt[name] = _cnt.get(name, 0) + 1
            if f"{name}:{_cnt[name]}" == _stop or name == _stop:
                P.enabled = False

        stage("setup")
        ring_ctr = [0]

        def load_slab(l, s):
            kc, ncol = SLAB_SH[s]
            n = kc * ncol
            slot = ring_ctr[0] % NRING
            ring_ctr[0] += 1
            dma("sp", ring[slot][:, 0:n], wbf[l, s, :, 0:n], [f"wbf{l}_{sgrp(s)}"], [f"ring{slot}"], f"rg{slot}")
            view = ring[slot][:, 0:n].rearrange("p (k c) -> p k c", k=kc)
            return view, f"ring{slot}"

        bank_ctr = [0]

        def next_bank():
            b = (0, 1, 2, 3, 5, 6)[bank_ctr[0] % 6]
            bank_ctr[0] += 1
            return PB[b], f"ps{b}"

        def pcol(l, off, m=None):
            c = l * PL + off + (0 if m is None else m)
            return prm[:, c:c + 1]

        sq_ctr = [0]

        def stats_rstd(src_chunks, src_keys, nd, rs_idx, statbank, statkey, ntok, sq_eng="act"):
            n = len(src_chunks)
            for i, (ap, key) in enumerate(zip(src_chunks, src_keys)):
                slot = sq_ctr[0] % 2
                sq_ctr[0] += 1
                if sq_eng == "act":
                    act(sqb[:, slot, 0:ntok], ap, AF.Square, [key], [f"sqb{slot}"])
                else:
                    tt(sq_eng, sqb[:, slot, 0:ntok], ap, ap, ALU.mult, [key], [f"sqb{slot}"])
                mm(statbank[:, 0:ntok], onesb, sqb[:, slot, 0:ntok], i == 0, i == n - 1,
                   [f"sqb{slot}", "cstb"], [statkey])
            act(rs[:, rs_idx, 0:ntok], statbank[:, 0:ntok], AF.Ln, [statkey], [f"rs{rs_idx}"], bias=float(nd * EPS))
            act(rs[:, rs_idx, 0:ntok], rs[:, rs_idx, 0:ntok], AF.Exp, [f"rs{rs_idx}"], [f"rs{rs_idx}"], scale=-0.5)

        def prenorm(l, goff, ntok):
            stats_rstd([h[:, c, 0:ntok] for c in range(8)], [f"h{c}" for c in range(8)], 1024, 0, PB[4], "ps4", ntok)
            for c in range(8):
                stt("dve", hn[:, c, 0:ntok], h[:, c, 0:ntok], pcol(l, goff, c), rs[:, 0, 0:ntok], ALU.mult, ALU.mult,
                    [f"h{c}", "rs0", "prm"], [f"hn{c}"])

        def mix_square(m, ntok):
            slot = sq_ctr[0] % 2
            sq_ctr[0] += 1
            act(sqb[:, slot, 0:ntok], mix[:, m, 0:ntok], AF.Square, [f"mix{m}"], [f"sqb{slot}"])
            return (m, slot)

        def mix_stat(pend, ntok):
            m, slot = pend
            mm(PB[4][:, 0:ntok], onesb, sqb[:, slot, 0:ntok], m == 0, m == 7, [f"sqb{slot}", "cstb"], ["ps4"])

        def postnorm_residual(l, goff, ntok, cast_hn=False):
            act(rs[:, 0, 0:ntok], PB[4][:, 0:ntok], AF.Ln, ["ps4"], ["rs0"], bias=float(1024 * EPS))
            act(rs[:, 0, 0:ntok], rs[:, 0, 0:ntok], AF.Exp, ["rs0"], ["rs0"], scale=-0.5)
            for c in range(8):
                stt("dve", mix[:, c, 0:ntok], mix[:, c, 0:ntok], pcol(l, goff, c), rs[:, 0, 0:ntok], ALU.mult, ALU.mult,
                    [f"mix{c}", "rs0", "prm"], [f"mix{c}"])
                tt("dve", h[:, c, 0:ntok], h[:, c, 0:ntok], mix[:, c, 0:ntok], ALU.add, [f"h{c}", f"mix{c}"], [f"h{c}"])
                if cast_hn:
                    cp("act", hn[:, c, 0:ntok], h[:, c, 0:ntok], [f"h{c}"], [f"hn{c}"])

        def layer(l, segs, ntok, pT_src, fix_col, snap, out_targets, post_ffn=None):
            dma("pool", pTb[:, :, 0:ntok], pT_src.rearrange("(c p) t -> p c t", p=128), [], ["pTb"], "pt")
            par = l % 2
            dma("sp", wdt[:, par, :], wbf[l, 3, :, 0:64], [f"wbf{l}_0"], [f"wdt{par}"], f"wd{par}")
            dma("sp", wpl[:, par, :], wbf[l, 5, :, 0:512], [f"wbf{l}_0"], [f"wpl{par}"], f"wp{par}")
            for (slot, sl, Q, c0) in segs:
                si = segs.index((slot, sl, Q, c0))
                eo = si * (3 + sl)
                uo = si * (15 + sl)
                cp("pool", xbc[:, :, eo:eo + 3], hcv[:, l, slot], [f"hcv{l}_{slot}"], [f"xbch{si}"])
                cp("pool", uex[:, :, uo:uo + 15], hpl[:, l, slot], [f"hpl{l}_{slot}"], [f"uexh{si}"])
            stage("hist")
            emit_conv(l, 1)
            prenorm(l, 0, ntok)
            stage("norm1")
            hn_keys = [f"hn{c}" for c in range(8)]

            def proj_chunk(slabv, slabk, cidx, src, src_keys, nk):
                ps, pk = next_bank()
                for kc in range(nk):
                    mm(ps[:, 0:ntok], slabv[:, kc, cidx * 128:(cidx + 1) * 128], src[:, kc, 0:ntok], kc == 0, kc == nk - 1,
                       [slabk, src_keys[kc]], [pk])
                return ps, pk

            chunks = []
            for si, (slot, sl, Q, c0) in enumerate(segs):
                for c in range(sl // Q):
                    chunks.append((len(chunks), si, slot, Q, c0 + c * Q, c))
            nch = len(chunks)
            Q = segs[0][2]
            W8 = nch * 8
            for (gi, si, slot, Q_, col, c) in chunks:
                for kc in range(8):
                    mm(P7a[0:Q, gi * 8:(gi + 1) * 8], hn[:, kc, col:col + Q], wdt[:, par, kc * 8:(kc + 1) * 8], kc == 0, kc == 7,
                       [f"hn{kc}", f"wdt{par}"], ["ps7"])
            v3 = lambda ap: ap.rearrange("p (c h) -> p c h", h=8)
            bc3 = lambda ap: ap.unsqueeze(1).to_broadcast([Q, nch, 8])
            tt("dve", v3(sm[0:Q, 0:W8]), v3(P7a[0:Q, 0:W8]), bc3(prm[0:Q, l * PL + 96:l * PL + 104]), ALU.add, ["ps7", "prm"], ["sm_a"])
            act(sm[0:Q, 0:W8], sm[0:Q, 0:W8], AF.Exp, ["sm_a"], ["sm_a"])
            act(sm[0:Q, 32:32 + W8], sm[0:Q, 0:W8], AF.Ln, ["sm_a"], ["sm_dt"], bias=1.0)
            tt("dve", v3(sm[0:Q, 64:64 + W8]), v3(sm[0:Q, 32:32 + W8]), bc3(prm[0:Q, l * PL + 104:l * PL + 112]), ALU.mult,
               ["sm_dt", "prm"], ["sm_dta"])
            mm(P7a[0:Q, 32:32 + W8], tri[0:Q, 0:Q], sm[0:Q, 64:64 + W8], True, True, ["cst", "sm_dta"], ["ps7"])
            mm(P7a[:, 64:64 + W8], ones[0:Q, 0:128], sm[0:Q, 64:64 + W8], True, True, ["cst", "sm_dta"], ["ps7"])
            cp("dve", sm[0:Q, 96:96 + W8], P7a[0:Q, 32:32 + W8], ["ps7"], ["sm_ac"])
            tt("dve", sm[0:Q, 128:128 + W8], P7a[0:Q, 64:64 + W8], sm[0:Q, 96:96 + W8], ALU.subtract, ["ps7", "sm_ac"], ["sm_dd"])
            act(sm[0:Q, 160:160 + W8], sm[0:Q, 128:128 + W8], AF.Exp, ["sm_dd"], ["sm_de"])
            act(sm[:, 192:192 + W8], P7a[:, 64:64 + W8], AF.Exp, ["ps7"], ["sm_dec"])

            for sidx, mbase in ((1, 0), (2, 4)):
                sv, sk = load_slab(l, sidx)
                for ci in range(4):
                    m = mbase + ci
                    ps, pk = proj_chunk(sv, sk, ci, hn, hn_keys, 8)
                    for si, (slot, sl, Q_, c0) in enumerate(segs):
                        eo = si * (3 + sl)
                        cp("act", xbc[:, m, eo + 3:eo + 3 + sl], ps[:, c0:c0 + sl], [pk], [f"xbc{m}_{si}"])
            stage("inproj")
            for m in (0, 4, 1, 5, 2, 6, 3, 7):
                for si, (slot, sl, Q_, c0) in enumerate(segs):
                    eo = si * (3 + sl)
                    a = (tA, tB, tC, tD)[(m % 2) + 2 * (m // 4)][:, c0:c0 + sl]
                    ak = ("tA", "tB", "tC", "tD")[(m % 2) + 2 * (m // 4)]
                    rk = [f"xbc{m}_{si}", f"xbch{si}", "prm"]
                    ce = "dve"
                    ts(ce, a, xbc[:, m, eo:eo + sl], pcol(l, 52, m * 4 + 0), pcol(l, 44, m), ALU.mult, ALU.add, rk, [ak])
                    for k in range(1, 4):
                        stt(ce, a, xbc[:, m, eo + k:eo + k + sl], pcol(l, 52, m * 4 + k), a, ALU.mult, ALU.add, rk + [ak], [ak])
                    if m < 4:
                        act(xs[:, m, c0:c0 + sl], a, AF.Silu, [ak], [f"xs{m}_{si}_{c}" for c in range(sl // Q_)])
                    else:
                        act(BCb[:, m - 4, c0:c0 + sl], a, AF.Silu, [ak], [f"BC{m - 4}_{si}"])
            sv, sk = load_slab(l, 0)
            for ci in range(4):
                ps, pk = proj_chunk(sv, sk, ci, hn, hn_keys, 8)
                act(zs[:, ci, 0:ntok], ps[:, 0:ntok], AF.Silu, [pk], [f"zs{ci}"])
            sv, sk = load_slab(l, 4)
            for ci in range(4):
                ps, pk = proj_chunk(sv, sk, ci, hn, hn_keys, 8)
                for si, (slot, sl, Q_, c0) in enumerate(segs):
                    uo = si * (15 + sl)
                    cp("act", uex[:, ci, uo + 15:uo + 15 + sl], ps[:, c0:c0 + sl], [pk], [f"uex{ci}_{si}"])
            for si, (slot, sl, Q_, c0) in enumerate(segs):
                eo = si * (3 + sl)
                cp("pool", hcv[:, l, slot], xbc[:, :, eo + sl:eo + sl + 3],
                   [f"xbc{m}_{si}" for m in range(8)] + [f"xbch{si}"], [f"hcv{l}_{slot}"])
                if snap is not None:
                    dma("pool", out_targets["conv"][si](snap, l), hcv[:, l, slot], [f"hcv{l}_{slot}"], [], "oc")
            stage("conv")
            for m in (2, 0, 3, 1):
                w = POOL_W[m]
                for si, (slot, sl, Q_, c0) in enumerate(segs):
                    uo = si * (15 + sl)
                    E = 15 + sl
                    u = uex[:, m, uo:uo + E]
                    uk = [f"uex{m}_{si}", f"uexh{si}"]
                    src, srck = u, uk
                    pe_ = "dve" if m < 2 else "pool"
                    bufs = [(tA, "tA"), (tB, "tB")] if m < 2 else [(tC, "tC"), (tD, "tD")]
                    step = 1
                    lo = 0
                    bi = 0
                    while step < w:
                        lo = lo + step
                        dst, dk = bufs[bi]
                        tt(pe_, dst[:, lo:E], src[:, lo:E], src[:, lo - step:E - step], ALU.add, srck, [dk])
                        src, srck = dst, [dk]
                        bi ^= 1
                        step *= 2
                    stt("dve", hn[:, 4 + m, c0:c0 + sl], src[:, 15:E], 1.0 / w, u[:, 15:E], ALU.mult, ALU.subtract,
                        srck + uk, [f"hn{4 + m}"])
                    if fix_col is not None:
                        tt("dve", sm[:, 224:240], src[:, 15:31], cst[:, fix_col + m * 16:fix_col + (m + 1) * 16], ALU.mult,
                           srck + ["cst"], ["smfix"])
                        tt("dve", hn[:, 4 + m, c0:c0 + 16], sm[:, 224:240], u[:, 15:31], ALU.subtract,
                           ["smfix"] + uk, [f"hn{4 + m}"])
                ps, pk = next_bank()
                mm(ps[:, 0:ntok], wpl[:, par, m * 128:(m + 1) * 128], hn[:, 4 + m, 0:ntok], True, True,
                   [f"wpl{par}", f"hn{4 + m}"], [pk])
                ts("dve", hn[:, 4 + m, 0:ntok], ps[:, 0:ntok], pcol(l, 84, m), pcol(l, 88, m), ALU.add, ALU.mult,
                   [pk, "prm"], [f"hn{4 + m}"])
            for si, (slot, sl, Q_, c0) in enumerate(segs):
                uo = si * (15 + sl)
                cp("pool", hpl[:, l, slot], uex[:, :, uo + sl:uo + sl + 15],
                   [f"uex{m}_{si}" for m in range(4)] + [f"uexh{si}"], [f"hpl{l}_{slot}"])
                if snap is not None:
                    dma("pool", out_targets["pool"][si](snap, l), hpl[:, l, slot], [f"hpl{l}_{slot}"], [], "op")
            stage("pool")
            hb = 512 // Q
            nb = 8 // hb

            def ssd_front(gi, si, slot, Q_, col, c):
                p = gi % 2
                cs = slice(col, col + Q)
                dta = sm[0:Q, 64 + gi * 8:64 + gi * 8 + 8]
                acs = sm[0:Q, 96 + gi * 8:96 + gi * 8 + 8]
                bA, LT, eRp, Cs = bufA[p], LTb[p], eR[p], Csb[p]
                Rv = []
                for b in range(nb):
                    bank = PB[5 + b]
                    for hh in range(hb * b, hb * (b + 1)):
                        mm(bank[:, (hh - hb * b) * Q:(hh - hb * b + 1) * Q], dta[:, hh:hh + 1].to_broadcast([Q, 128]), tri[0:Q, 0:Q],
                           True, True, ["cst", "sm_dta"], [f"ps{5 + b}"])
                    Rv.append((bank[:, 0:hb * Q].rearrange("p (h q) -> p h q", q=Q), f"ps{5 + b}"))
                for b, (R, Rk) in enumerate(Rv):
                    hsl = slice(hb * b, hb * (b + 1))
                    for hh in range(hb * b, hb * (b + 1)):
                        stt("dve", bA[0:Q, hh, 0:Q], R[0:Q, hh - hb * b, :], acs[:, hh:hh + 1], negm[0:Q, 0:Q], ALU.subtract, ALU.add,
                            [Rk, "sm_ac", "cst"], [f"bufA{p}_{b}"])
                    act(LT[0:Q, hsl, 0:Q], bA[0:Q, hsl, 0:Q], AF.Exp, [f"bufA{p}_{b}"], [f"LT{p}_{b}"])
                    act(eRp[:, hsl, 0:Q], R, AF.Exp, [Rk], [f"eR_{b}"])
                LTk = [f"LT{p}_{b}" for b in range(nb)]
                eRk = [f"eR_{b}" for b in range(nb)]
                for g in range(2):
                    mm(P7a[0:Q, 128 + g * Q:128 + (g + 1) * Q], BCb[:, g, cs], BCb[:, 2 + g, cs], True, True,
                       [f"BC{g}_{si}", f"BC{2 + g}_{si}"], ["ps7"])
                for g in range(2):
                    tt("dve", LT[0:Q, 4 * g:4 * g + 4, 0:Q], LT[0:Q, 4 * g:4 * g + 4, 0:Q],
                       P7a[0:Q, 128 + g * Q:128 + (g + 1) * Q].unsqueeze(1).to_broadcast([Q, 4, Q]), ALU.mult,
                       LTk + ["ps7"], LTk)
                for g in range(2):
                    tt("pool", Cs[:, 4 * g:4 * g + 4, 0:Q], eRp[:, 4 * g:4 * g + 4, 0:Q],
                       BCb[:, 2 + g, cs].unsqueeze(1).to_broadcast([128, 4, Q]), ALU.mult,
                       eRk + [f"BC{2 + g}_{si}"], [f"Csb{p}"])
                for m in range(4):
                    tr(PB[4][0:Q, m * 128:(m + 1) * 128], xs[:, m, cs], ident, [f"xs{m}_{si}_{c}", "cst"], ["ps4"])
                tt("dve", xdtb[p][0:Q, :].rearrange("p (h d) -> p h d", d=64),
                   PB[4][0:Q, :].rearrange("p (h d) -> p h d", d=64),
                   sm[0:Q, 32 + gi * 8:32 + gi * 8 + 8].unsqueeze(2).to_broadcast([Q, 8, 64]), ALU.mult, ["ps4", "sm_dt"], [f"xdtb{p}"])
                tt("pool", xdtdb[p][0:Q, :].rearrange("p (h d) -> p h d", d=64),
                   xdtb[p][0:Q, :].rearrange("p (h d) -> p h d", d=64),
                   sm[0:Q, 160 + gi * 8:160 + gi * 8 + 8].unsqueeze(2).to_broadcast([Q, 8, 64]), ALU.mult,
                   [f"xdtb{p}", "sm_de"], [f"xdtdb{p}"])
                for g in range(2):
                    tr(P7b[0:Q, g * 128:(g + 1) * 128], BCb[:, g, cs], identb, [f"BC{g}_{si}", "cstb"], ["ps3"])
                cp("act", BTb[p][0:Q, :, :], P7b[0:Q, :].rearrange("p (g n) -> p g n", g=2), ["ps3"], [f"BTb{p}"])

            def ssd_back(gi, si, slot, Q_, col, c, last_in_seg):
                p = gi % 2
                cs = slice(col, col + Q)
                LT, Cs = LTb[p], Csb[p]
                LTk = [f"LT{p}_{b}" for b in range(nb)]
                for hh in range(8):
                    pr = hh // 2
                    if Q == 128:
                        yp = PB[hh // 4][:, (hh % 4) * Q:(hh % 4 + 1) * Q]
                        yk = f"ps{hh // 4}"
                    else:
                        yp = PB[0][:, hh * Q:(hh + 1) * Q]
                        yk = "ps0"
                    mm(yp, xdtb[p][0:Q, pr * 128:(pr + 1) * 128], LT[0:Q, hh, 0:Q], True, False, [f"xdtb{p}"] + LTk, [yk])
                    mm(yp, hTb[:, slot, pr * 128:(pr + 1) * 128], Cs[:, hh, 0:Q], False, False, [f"hTb{slot}", f"Csb{p}"], [yk])
                    mm(yp, dD[:, l, pr, :], xs[:, pr, cs], False, True, ["dD", f"xs{pr}_{si}_{c}"], [yk])
                nbk = 2 if Q == 128 else 1
                ppb = 4 // nbk
                for bk in range(nbk):
                    v = PB[bk][:, :].rearrange("p (a two q) -> p a two q", two=2, q=Q)
                    keys = [f"xs{pr}_{si}_{c}" for pr in range(bk * ppb, (bk + 1) * ppb)]
                    zkeys = [f"zs{pr}" for pr in range(bk * ppb, (bk + 1) * ppb)]
                    for par_ in range(2):
                        po = 64 * par_
                        tt("dve", zs[po:po + 64, bk * ppb:(bk + 1) * ppb, cs], v[po:po + 64, :, par_, :],
                           zs[po:po + 64, bk * ppb:(bk + 1) * ppb, cs], ALU.mult, [f"ps{bk}"] + zkeys, zkeys)
                for g in range(2):
                    mm(PB[2][:, g * 256:(g + 1) * 256], BTb[p][0:Q, g, :], xdtdb[p][0:Q, g * 256:(g + 1) * 256], True, True,
                       [f"BTb{p}", f"xdtdb{p}"], ["ps2"])
                hv = hT[:, l, slot, :].rearrange("p (h d) -> p h d", d=64)
                tt("dve", hv, hv, sm[:, 192 + gi * 8:192 + gi * 8 + 8].unsqueeze(2).to_broadcast([128, 8, 64]), ALU.mult,
                   [f"hT{l}_{slot}", "sm_dec"], [f"hT{l}_{slot}"])
                tt("dve", hT[:, l, slot, :], hT[:, l, slot, :], PB[2][:, :], ALU.add, [f"hT{l}_{slot}", "ps2"], [f"hT{l}_{slot}"])
                if not last_in_seg:
                    cp("act", hTb[:, slot, :], hT[:, l, slot, :], [f"hT{l}_{slot}"], [f"hTb{slot}"])
                elif snap is not None:
                    dma("pool", out_targets["ssm"][si](snap, l), hT[:, l, slot, :], [f"hT{l}_{slot}"], [], "os")

            for si, (slot, sl, Q_, c0) in enumerate(segs):
                cp("act", hTb[:, slot, :], hT[:, l, slot, :], [f"hT{l}_{slot}"], [f"hTb{slot}"])
            ssd_front(*chunks[0])
            for i, ch in enumerate(chunks):
                if i + 1 < nch:
                    ssd_front(*chunks[i + 1])
                last_in_seg = (i + 1 == nch) or (chunks[i + 1][1] != ch[1])
                ssd_back(*ch, last_in_seg)
            stage("ssd")
            xs_keys = lambda m: [f"xs{m}_{si}_{c}" for si, sg in enumerate(segs) for c in range(sg[1] // sg[2])]
            for g in range(2):
                stats_rstd([zs[:, 2 * g + i, 0:ntok] for i in range(2)], [f"zs{2 * g + i}" for i in range(2)], 256, g,
                           PB[4 + g], f"ps{4 + g}", ntok)
            for m in range(4):
                stt("dve", hn[:, m, 0:ntok], zs[:, m, 0:ntok], pcol(l, 40, m), rs[:, m // 2, 0:ntok], ALU.mult, ALU.mult,
                    [f"zs{m}", f"rs{m // 2}", "prm"], [f"hn{m}"])
            mk = [f"hn{c}" for c in range(8)]
            pend = None
            for sidx in (6, 7):
                sv, sk = load_slab(l, sidx)
                for ci in range(4):
                    m = (sidx - 6) * 4 + ci
                    ps, pk = proj_chunk(sv, sk, ci, hn, mk, 8)
                    if pend is not None:
                        mix_stat(pend, ntok)
                    cp("act", mix[:, m, 0:ntok], ps[:, 0:ntok], [pk], [f"mix{m}"])
                    pend = mix_square(m, ntok)
            mix_stat(pend, ntok)
            postnorm_residual(l, 8, ntok)
            stage("mixer")
            emit_conv(l, 2)
            emit_conv(l + 1, 0)
            prenorm(l, 16, ntok)
            for j in range(6):
                gv, gk = load_slab(l, 8 + 2 * j)
                uv, uk_ = load_slab(l, 9 + 2 * j)
                for ci in range(4 if j < 5 else 2):
                    m = 4 * j + ci
                    psg, pkg = proj_chunk(gv, gk, ci, hn, hn_keys, 8)
                    psu, pku = proj_chunk(uv, uk_, ci, hn, hn_keys, 8)
                    sgb, sgk = ((tA, "tA"), (tB, "tB"))[m % 2]
                    act(sgb[:, 0:ntok], psg[:, 0:ntok], AF.Silu, [pkg], [sgk])
                    tt("dve", actb[:, m, 0:ntok], sgb[:, 0:ntok], psu[:, 0:ntok], ALU.mult, [sgk, pku], [f"actb{m}"])
            ak_ = [f"actb{m}" for m in range(22)]
            pend = None
            for m in range(8):
                dv, dk = load_slab(l, 20 + m)
                ps, pk = next_bank()
                for kc in range(22):
                    mm(ps[:, 0:ntok], dv[:, kc, :], actb[:, kc, 0:ntok], kc == 0, kc == 21, [dk, ak_[kc]], [pk])
                if pend is not None:
                    mix_stat(pend, ntok)
                cp("act", mix[:, m, 0:ntok], ps[:, 0:ntok], [pk], [f"mix{m}"])
                pend = mix_square(m, ntok)
            mix_stat(pend, ntok)
            postnorm_residual(l, 24, ntok, cast_hn=True)
            if post_ffn is not None:
                post_ffn()
            stage("ffn")
            ppv, ppk = load_slab(l, 30)
            pend = None
            for sidx in (28, 29):
                sv, sk = load_slab(l, sidx)
                for ci in range(4):
                    m = (sidx - 28) * 4 + ci
                    psg, pkg = proj_chunk(sv, sk, ci, hn, hn_keys, 8)
                    pse, pke = next_bank()
                    for kc in range(2):
                        mm(pse[:, 0:ntok], ppv[:, kc, m * 128:(m + 1) * 128], pTb[:, kc, 0:ntok], kc == 0, kc == 1, [ppk, "pTb"], [pke])
                    sgb, sgk = ((tA, "tA"), (tB, "tB"))[m % 2]
                    act(sgb[:, 0:ntok], psg[:, 0:ntok], AF.Sigmoid, [pkg], [sgk])
                    tt("dve", mix[:, m, 0:ntok], sgb[:, 0:ntok], pse[:, 0:ntok], ALU.mult, [sgk, pke], [f"mix{m}"])
            for m in range(8):
                mix_stat(mix_square(m, ntok), ntok)
            postnorm_residual(l, 32, ntok)
            stage("ple")

        hkeys = [f"h{c}" for c in range(8)]

        fB = prm[:, L * PL + 0:L * PL + 1]
        keep = prm[:, L * PL + 1:L * PL + 2]

        def run_tile(x_src, g_src, y_dst, ntok, segs, pT_of_layer, fix_col, snap, out_targets):
            dma("pool", h[:, :, 0:ntok], x_src.rearrange("(c p) t -> p c t", p=128), [], hkeys, "xin")
            if g_src is not None:
                gv, gk = g_src
                dma("pool", mix[:, :, 0:ntok], gv.rearrange("(c p) t -> p c t", p=128), [gk], [f"mix{c}" for c in range(8)], "gin")
                for c in range(8):
                    stt("dve", h[:, c, 0:ntok], mix[:, c, 0:ntok], fB, h[:, c, 0:ntok], ALU.mult, ALU.add,
                        [f"mix{c}", f"h{c}", "prm"], [f"h{c}"])
            for l in range(L):
                layer(l, segs, ntok, pT_of_layer(l), fix_col, snap, out_targets)
            dma("pool", y_dst.rearrange("(c p) t -> p c t", p=128), h[:, :, 0:ntok], hkeys, [], "yout")

        xst = actb[:, 0:16, :].bitcast(F32).rearrange("p (a two) n -> p a (two n)", two=2)
        xst_keys = [f"actb{m}" for m in range(16)]

        def prefetch_x(k):
            cols = slice(k * TT, (k + 1) * TT)
            dma("pool", xst, xT[:, cols].rearrange("(c p) t -> p c t", p=128), [], xst_keys, "xin")

        def exchange(x_t, g_t, ntok, xk, gk):
            dma("pool", x_t.ap().rearrange("(c p) t -> p c t", p=128), h[:, :, 0:ntok], hkeys + [gk], [xk], "xout")
            P.add("pool", lambda e: e.collective_compute("AllGather", ALU.bypass, replica_groups=PAIRS,
                                                          ins=[x_t.ap().opt()], outs=[g_t.ap().opt()]),
                  [xk], [gk], dma="cc", inc=1)

        segs = [(0, 64, 64, 0), (1, 64, 64, 64)]
        tg = {
            "ssm": [lambda sn, l, b=b: ssm_s[sn, l, b] for b in range(2)],
            "conv": [lambda sn, l, b=b: conv_s[sn, l, b] for b in range(2)],
            "pool": [lambda sn, l, b=b: pool_s[sn, l, b] for b in range(2)],
        }
        for sstep in range(2):
            for l in range(L):
                for b in range(2):
                    dma("pool", hT[:, l, b, :], hs0[l, b], [], [f"hT{l}_{b}"], "st")
                    dma("pool", hcv[:, l, b], cv0[l, b], [], [f"hcv{l}_{b}"], "st")
                    dma("pool", hpl[:, l, b], pl0[l, b], [], [f"hpl{l}_{b}"], "st")
            run_tile(xsT, None if sstep == 0 else (gs_t.ap()[0:1024, :], "gs"), ysT[sstep], 128, segs,
                     lambda l: psT[l], None, sstep, tg)
            if sstep == 0:
                exchange(sx_t, gs_t, 128, "sx", "gs")
        for l in range(L):
            P.add("dve", lambda e, l=l: e.memset(hT[:, l, 0, :], 0.0), [], [f"hT{l}_0"])
            P.add("dve", lambda e, l=l: e.memset(hcv[:, l, 0], 0.0), [], [f"hcv{l}_0"])
            P.add("dve", lambda e, l=l: e.memset(hpl[:, l, 0], 0.0), [], [f"hpl{l}_0"])
        tg = {
            "ssm": [lambda sn, l: ssm_p[sn, l]],
            "conv": [lambda sn, l: conv_p[sn, l]],
            "pool": [lambda sn, l: pool_p[sn, l]],
        }
        NS = n_ptiles + LAG
        seg_p = [(0, TT, 128, 0)]
        prefetch_x(0)
        for k in range(NS):
            cols = slice(k * TT, (k + 1) * TT)
            snap = {n_ptiles - 1: 0, NS - 1: 1}.get(k)
            fix_col = 512 + 64 * k if k <= LAG else None
            for c in range(8):
                if k >= LAG:
                    stt("dve", h[:, c, :], mix[:, c, :], fB, xst[:, c, :], ALU.mult, ALU.add,
                        [f"mix{c}", "prm"] + xst_keys, [f"h{c}"])
                else:
                    cp("dve", h[:, c, :], xst[:, c, :], xst_keys, [f"h{c}"])
            for l in range(L):
                pf = (lambda k=k: prefetch_x(k + 1)) if (l == L - 1 and k + 1 < NS) else None
                layer(l, seg_p, TT, ppT[l][:, cols], fix_col, snap, tg, post_ffn=pf)
            if k == LAG - 1:
                for l in range(L):
                    ts1("dve", hT[:, l, 0, :], hT[:, l, 0, :], keep, ALU.mult, [f"hT{l}_0", "prm"], [f"hT{l}_0"])
                    ts1("dve", hcv[:, l, 0], hcv[:, l, 0], keep, ALU.mult, [f"hcv{l}_0", "prm"], [f"hcv{l}_0"])
                    ts1("dve", hpl[:, l, 0], hpl[:, l, 0], keep, ALU.mult, [f"hpl{l}_0", "prm"], [f"hpl{l}_0"])
            px, pk_ = px_t[k % 2], f"px{k % 2}"
            dma("pool", px.ap().rearrange("(c p) t -> p c t", p=128), h[:, :, :], hkeys, [pk_], "xout")
            if k + 1 >= LAG and k + 1 < NS:
                gn = gp_t[(k + 1) % 2]
                dma("pool", mix[:, :, :], gn.ap()[0:1024, :].rearrange("(c p) t -> p c t", p=128), [f"gp{(k + 1) % 2}"],
                    [f"mix{c}" for c in range(8)], "gin")
            dma("pool", yT[:, cols], px.ap(), [pk_], [], "yout")
            if k < n_ptiles:
                P.add("pool", lambda e, px=px, k=k: e.collective_compute("AllGather", ALU.bypass, replica_groups=PAIRS,
                                                                         ins=[px.ap().opt()], outs=[gp_t[k % 2].ap().opt()]),
                      [pk_], [f"gp{k % 2}"], dma="cc", inc=1)
        P.emit(nc, block, sems)
    return nc


def _slab(W, c0, ncol):
    K = W.shape[0]
    kc = K // 128
    a = W[:, c0:c0 + ncol].reshape(kc, 128, ncol).transpose(1, 0, 2).reshape(128, kc * ncol)
    out = np.zeros((128, SLABW), np.float32)
    out[:, :kc * ncol] = a
    return out


def make_slabs(inp, layers):
    L = len(layers)
    wsl = np.zeros((L, NSLAB, 128, SLABW), np.float32)
    for li, l in enumerate(layers):
        w_in = np.asarray(inp["w_in"][l])
        wsl[li, 0] = _slab(w_in, 0, 512)
        wsl[li, 1] = _slab(w_in, 512, 512)
        wsl[li, 2] = _slab(w_in, 1024, 512)
        wsl[li, 3] = _slab(w_in, 1536, 8)
        wsl[li, 4] = _slab(w_in, 1544, 512)
        pw = np.asarray(inp["pool_w"][l])
        wsl[li, 5, :, :512] = pw.transpose(1, 0, 2).reshape(128, 512)
        w_out = np.asarray(inp["w_out"][l])
        wsl[li, 6] = _slab(w_out, 0, 512)
        wsl[li, 7] = _slab(w_out, 512, 512)
        wg = np.asarray(inp["w_gate"][l])
        wu = np.asarray(inp["w_up"][l])
        for j in range(6):
            ncol = 512 if j < 5 else 256
            wsl[li, 8 + 2 * j] = _slab(wg, j * 512, ncol)
            wsl[li, 9 + 2 * j] = _slab(wu, j * 512, ncol)
        wd = np.asarray(inp["w_down"][l])
        for m in range(8):
            wsl[li, 20 + m] = _slab(wd, m * 128, 128)
        wpg = np.asarray(inp["w_ple_gate"][l])
        wsl[li, 28] = _slab(wpg, 0, 512)
        wsl[li, 29] = _slab(wpg, 512, 512)
        wsl[li, 30] = _slab(np.asarray(inp["w_ple_proj"][l]), 0, 1024)
    return wsl


def make_params(inp, layers, role):
    L = len(layers)
    prm = np.zeros((128, L * PL + 8), np.float32)
    fm = lambda v: np.asarray(v).reshape(-1, 128).T
    for li, l in enumerate(layers):
        b = li * PL
        prm[:, b + 0:b + 8] = fm(inp["pre_mix_g"][l])
        prm[:, b + 8:b + 16] = fm(inp["post_mix_g"][l])
        prm[:, b + 16:b + 24] = fm(inp["pre_ffn_g"][l])
        prm[:, b + 24:b + 32] = fm(inp["post_ffn_g"][l])
        prm[:, b + 32:b + 40] = fm(inp["ple_norm_g"][l])
        prm[:, b + 40:b + 44] = fm(inp["ssm_norm_g"][l])
        prm[:, b + 44:b + 52] = fm(inp["conv_b"][l])
        cw = np.asarray(inp["conv_w"][l])
        for m in range(8):
            for k in range(4):
                prm[:, b + 52 + m * 4 + k] = cw[k, m * 128:(m + 1) * 128]
        prm[:, b + 84:b + 88] = fm(np.asarray(inp["pool_b"][l]).reshape(-1))
        prm[:, b + 88:b + 92] = fm(inp["pool_scale"][l])
        dsk = np.asarray(inp["d_skip"][l])
        prm[:, b + 92:b + 96] = np.repeat(dsk, 64).reshape(4, 128).T
        prm[:, b + 96:b + 104] = np.broadcast_to(np.asarray(inp["dt_bias"][l])[None, :], (128, 8))
        prm[:, b + 104:b + 112] = np.broadcast_to(np.asarray(inp["a_log"][l])[None, :], (128, 8))
    prm[:, L * PL + 0] = 1.0 if role == 1 else 0.0
    prm[:, L * PL + 1] = 0.0 if role == 1 else 1.0
    return prm


def make_consts(role):
    c = np.zeros((128, 512 + 64 * (LAG + 1)), np.float32)
    c[:, 0:128] = np.eye(128, dtype=np.float32)
    c[:, 128:256] = np.triu(np.ones((128, 128), np.float32))
    c[:, 256:384] = 1.0
    c[:, 384:512] = np.where(np.arange(128)[None, :] >= np.arange(128)[:, None], 0.0, -30000.0)
    for step in range(LAG + 1):
        for m, w in enumerate(POOL_W):
            if step == role * LAG:
                v = 1.0 / np.minimum(np.arange(16) + 1, w)
            else:
                v = np.full(16, 1.0 / w)
            c[:, 512 + step * 64 + m * 16:512 + step * 64 + (m + 1) * 16] = v[None, :]
    return c


def prep_core_inputs(inp, layers, role, prompt_b, n_ptiles, sample_bs, shared):
    d = dict(shared)
    L = len(layers)
    ntok = n_ptiles * TT
    NPT = (n_ptiles + LAG) * TT
    xT = np.zeros((1024, NPT), np.float32)
    xsT = np.zeros((1024, 128), np.float32)
    ppT = np.zeros((L, 256, NPT), np.float32)
    off = 0 if role == 0 else LAG * TT
    if role == 0:
        xT[:, 0:ntok] = np.asarray(inp["x_prompt"][prompt_b, 0:ntok]).T
        xsT[:] = np.concatenate([np.asarray(inp["x_sample"][b]) for b in sample_bs], 0).T
    ppT[:, :, off:off + ntok] = np.asarray(inp["p_prompt"])[layers][:, prompt_b, 0:ntok].transpose(0, 2, 1)
    d["xT"] = xT
    d["xsT"] = xsT
    d["ppT"] = ppT
    d["psT"] = np.ascontiguousarray(
        np.concatenate([np.asarray(inp["p_sample"])[layers][:, b] for b in sample_bs], 1).transpose(0, 2, 1))
    ss = np.asarray(inp["state_ssm"])[layers][:, sample_bs]
    d["hs0"] = np.ascontiguousarray(ss.reshape(L, 2, 512, 128).transpose(0, 1, 3, 2))
    sc = np.asarray(inp["state_conv"])[layers][:, sample_bs]
    d["cv0"] = np.ascontiguousarray(sc.reshape(L, 2, 3, 8, 128).transpose(0, 1, 4, 3, 2))
    sp = np.asarray(inp["state_pool"])[layers][:, sample_bs]
    d["pl0"] = np.ascontiguousarray(sp.reshape(L, 2, 15, 4, 128).transpose(0, 1, 4, 3, 2))
    return d


def unpack_core(res, L, role, n_ptiles):
    sn = role
    o = {}
    ntok = n_ptiles * TT
    off = 0 if role == 0 else LAG * TT
    o["y"] = res["yT"][:, off:off + ntok].T
    o["ys"] = res["ysT"][sn].T.reshape(2, 64, 1024)
    o["ssm_p"] = res["ssm_p"][sn].transpose(0, 2, 1).reshape(L, 8, 64, 128)
    o["conv_p"] = res["conv_p"][sn].transpose(0, 3, 2, 1).reshape(L, 3, 1024)
    o["pool_p"] = res["pool_p"][sn].transpose(0, 3, 2, 1).reshape(L, 15, 512)
    o["ssm_s"] = res["ssm_s"][sn].transpose(0, 1, 3, 2).reshape(L, 2, 8, 64, 128)
    o["conv_s"] = res["conv_s"][sn].transpose(0, 1, 4, 3, 2).reshape(L, 2, 3, 1024)
    o["pool_s"] = res["pool_s"][sn].transpose(0, 1, 4, 3, 2).reshape(L, 2, 15, 512)
    return o


_NC_CACHE = {}


def kernel(**inp):
    LL = DEPTH // 2
    n_ptiles = 8192 // TT
    key = (n_ptiles, LL)
    if key not in _NC_CACHE:
        _NC_CACHE[key] = build_program(n_ptiles, LL)
    nc = _NC_CACHE[key]
    stage_layers = [list(range(0, LL)), list(range(LL, DEPTH))]
    shared = []
    for role in range(2):
        shared.append({"wsl": make_slabs(inp, stage_layers[role]), "prm": make_params(inp, stage_layers[role], role),
                       "cst": make_consts(role)})
    in_maps = []
    for c in range(8):
        b, role = c // 2, c % 2
        in_maps.append(prep_core_inputs(inp, stage_layers[role], role, b, n_ptiles, [2 * b, 2 * b + 1], shared[role]))
    res = run_bass_kernel_spmd(nc, in_maps, core_ids=list(range(8)))
    outs = [[unpack_core(res.results[2 * b + role], LL, role, n_ptiles) for role in range(2)] for b in range(4)]
    cat_l = lambda k, b: np.concatenate([outs[b][0][k], outs[b][1][k]], 0)
    y_prompt = np.stack([outs[b][1]["y"] for b in range(4)], 0).astype(np.float32)
    y_sample = np.concatenate([outs[b][1]["ys"] for b in range(4)], 0).astype(np.float32)
    ssm_p = np.stack([cat_l("ssm_p", b) for b in range(4)], 1).astype(np.float32)
    conv_p = np.stack([cat_l("conv_p", b) for b in range(4)], 1).astype(np.float32)
    pool_p = np.stack([cat_l("pool_p", b) for b in range(4)], 1).astype(np.float32)
    ssm_s = np.concatenate([cat_l("ssm_s", b) for b in range(4)], 1).astype(np.float32)
    conv_s = np.concatenate([cat_l("conv_s", b) for b in range(4)], 1).astype(np.float32)
    pool_s = np.concatenate([cat_l("pool_s", b) for b in range(4)], 1).astype(np.float32)
    return (np.ascontiguousarray(y_prompt), np.ascontiguousarray(y_sample), np.ascontiguousarray(ssm_p),
            np.ascontiguousarray(conv_p), np.ascontiguousarray(pool_p), np.ascontiguousarray(ssm_s),
            np.ascontiguousarray(conv_s), np.ascontiguousarray(pool_s))
```

```python
import numpy as np
from contextlib import ExitStack
import concourse.bass as bass
import concourse.mybir as mybir
from concourse.bass_utils import run_bass_kernel_spmd

F32 = mybir.dt.float32
BF16 = mybir.dt.bfloat16
AF = mybir.ActivationFunctionType
ALU = mybir.AluOpType

D_MODEL = 1024
DEPTH = 4
D_SSM = 512
D_FF = 2816
D_PLE = 256
IN_COLS = 2056
EPS = 1e-6
POOL_W = (2, 4, 8, 16)
TT = 512
NSLAB = 31
SLABW = 4096
PL = 112
NRING = 4

SEM_EPOCH = 30000
PAIRS = [[0, 1], [2, 3], [4, 5], [6, 7]]
LAG = 2


class _Op:
    __slots__ = ("eng", "fn", "dma", "deps", "sig", "signal", "need", "inc")

    def __init__(self, eng, fn, dma):
        self.eng = eng
        self.fn = fn
        self.dma = dma
        self.deps = None
        self.sig = None
        self.signal = False


class Prog:
    def __init__(self):
        self.ops = []
        self.lastw = {}
        self.readers = {}

    enabled = True

    def add(self, eng, fn, reads=(), writes=(), dma=None, inc=16):
        if not self.enabled:
            return None
        idx = len(self.ops)
        op = _Op(eng, fn, dma)
        op.inc = inc
        agent = ("dma", dma) if dma else eng
        raw = set()
        oth = set()
        lastw = self.lastw
        readers = self.readers
        for k in reads:
            w = lastw.get(k)
            if w is not None:
                raw.add(w)
        for k in writes:
            w = lastw.get(k)
            if w is not None:
                oth.add(w)
            rd = readers.get(k)
            if rd:
                oth.update(rd.values())
        for k in reads:
            rd = readers.get(k)
            if rd is None:
                readers[k] = {agent: idx}
            else:
                rd[agent] = idx
        for k in writes:
            lastw[k] = idx
            readers[k] = {}
        deps = set(raw)
        ops = self.ops
        for d in oth:
            if d in raw:
                continue
            od = ops[d]
            if eng == "pe" and od.eng == "pe" and dma is None and od.dma is None:
                continue
            deps.add(d)
        op.deps = deps
        ops.append(op)
        return idx

    def emit(self, nc, block, sem_pool):
        ops = self.ops
        for op in ops:
            for d in op.deps:
                ops[d].signal = True
        counters = {}
        for op in ops:
            need = {}
            for d in op.deps:
                k, v = ops[d].sig
                if k[0] == "dma":
                    v = counters[k]
                if need.get(k, 0) < v:
                    need[k] = v
            op.need = need
            if op.dma:
                key = ("dma", op.dma)
                c = counters.get(key, 0) + op.inc
                counters[key] = c
                op.sig = (key, c)
            elif op.signal:
                c = counters.get(op.eng, 0) + 1
                counters[op.eng] = c
                ep = (c - 1) // SEM_EPOCH
                op.sig = ((op.eng, ep), c - ep * SEM_EPOCH)
        semh = {}
        for op in ops:
            if op.sig is not None and op.sig[0] not in semh:
                semh[op.sig[0]] = sem_pool.pop()

        def make_stream(ename):
            def run(e):
                waited = {}
                for op in ops:
                    if op.eng != ename:
                        continue
                    pend = [(k, v) for k, v in op.need.items() if waited.get(k, 0) < v]
                    for k, v in pend:
                        waited[k] = v
                    emb = pend.pop() if (pend and op.inc != 1) else None
                    for k, v in pend:
                        e.wait_ge(semh[k], v)
                    ins = op.fn(e)
                    if emb is not None:
                        ins._wait_ge(semh[emb[0]], emb[1])
                    if op.sig is not None:
                        ins.then_inc(semh[op.sig[0]], op.inc if op.dma else 1)
                fin = {}
                for op in ops:
                    if op.eng == ename and op.dma:
                        k, v = op.sig
                        fin[k] = max(fin.get(k, 0), v)
                for k, v in fin.items():
                    if waited.get(k, 0) < v:
                        e.wait_ge(semh[k], v)
            return run

        block.tensor(make_stream("pe"))
        block.scalar(make_stream("act"))
        block.vector(make_stream("dve"))
        block.gpsimd(make_stream("pool"))
        block.sync(make_stream("sp"))
        return counters


def slab_shapes():
    sh = [None] * NSLAB
    sh[0] = (8, 512)
    sh[1] = (8, 512)
    sh[2] = (8, 512)
    sh[3] = (8, 8)
    sh[4] = (8, 512)
    sh[5] = (4, 128)
    sh[6] = (8, 512)
    sh[7] = (8, 512)
    for j in range(6):
        nc_ = 512 if j < 5 else 256
        sh[8 + 2 * j] = (8, nc_)
        sh[9 + 2 * j] = (8, nc_)
    for m in range(8):
        sh[20 + m] = (22, 128)
    sh[28] = (8, 512)
    sh[29] = (8, 512)
    sh[30] = (2, 1024)
    return sh


SLAB_SH = slab_shapes()
SLAB_N = [kc * nc_ for kc, nc_ in SLAB_SH]
SLAB_OFF = [128 * sum(SLAB_N[:i]) for i in range(NSLAB + 1)]
WTOT = SLAB_OFF[NSLAB]


def build_program(n_ptiles, depth):
    NPT = (n_ptiles + LAG) * TT
    L = depth
    nc = bass.Bass("TRN2", target_bir_lowering=False)
    dram_in = lambda n, s, d=F32: nc.dram_tensor(n, s, d, kind="ExternalInput").ap()
    dram_out = lambda n, s, d=F32: nc.dram_tensor(n, s, d, kind="ExternalOutput").ap()
    xT = dram_in("xT", [1024, NPT])
    xsT = dram_in("xsT", [1024, 128])
    ppT = dram_in("ppT", [L, 256, NPT])
    psT = dram_in("psT", [L, 256, 128])
    hs0 = dram_in("hs0", [L, 2, 128, 512])
    cv0 = dram_in("cv0", [L, 2, 128, 8, 3])
    pl0 = dram_in("pl0", [L, 2, 128, 4, 15])
    wsl = dram_in("wsl", [L, WTOT])
    prm_d = dram_in("prm", [128, L * PL + 8])
    cst_d = dram_in("cst", [128, 512 + 64 * (LAG + 1)])
    wbf = nc.dram_tensor("wbf", [L, WTOT], BF16, kind="Internal").ap()

    def wslab(l, s_):
        n_ = SLAB_N[s_]
        return wbf[l, SLAB_OFF[s_]:SLAB_OFF[s_] + 128 * n_].rearrange("(p n) -> p n", p=128)

    sx_t = nc.dram_tensor("sx", [1024, 128], F32)
    gs_t = nc.dram_tensor("gs", [2048, 128], F32)
    px_t = [nc.dram_tensor(f"px{i}", [1024, TT], F32) for i in range(2)]
    gp_t = [nc.dram_tensor(f"gp{i}", [2048, TT], F32) for i in range(2)]
    yT = dram_out("yT", [1024, NPT])
    ysT = dram_out("ysT", [2, 1024, 128])
    ssm_p = dram_out("ssm_p", [2, L, 128, 512])
    conv_p = dram_out("conv_p", [2, L, 128, 8, 3])
    pool_p = dram_out("pool_p", [2, L, 128, 4, 15])
    ssm_s = dram_out("ssm_s", [2, L, 2, 128, 512])
    conv_s = dram_out("conv_s", [2, L, 2, 128, 8, 3])
    pool_s = dram_out("pool_s", [2, L, 2, 128, 4, 15])

    P = Prog()
    with ExitStack() as st:
        sb = lambda n, s, d=F32: st.enter_context(nc.sbuf_tensor(n, s, d))
        pst = lambda n, s, d=F32: st.enter_context(nc.psum_tensor(n, s, d))
        h = sb("h", [128, 8, TT])
        hn = sb("hn", [128, 8, TT], BF16)
        sqb = sb("sqb", [128, 2, TT], BF16)
        rs = sb("rs", [128, 2, TT])
        zs = sb("zs", [128, 4, TT])
        XW = 3 + TT
        xbc = sb("xbc", [128, 8, XW])
        xs = sb("xs", [128, 4, TT])
        BCb = sb("BCb", [128, 4, TT], BF16)
        UW = 15 + TT
        uex = sb("uex", [128, 4, UW])
        tA = sb("tA", [128, UW])
        tB = sb("tB", [128, UW])
        tC = sb("tC", [128, UW])
        tD = sb("tD", [128, UW])
        mix = sb("mix", [128, 8, TT])
        actb = sb("actb", [128, 22, TT], BF16)
        bufA = [sb(f"bufA{i}", [128, 8, 128]) for i in range(2)]
        LTb = [sb(f"LTb{i}", [128, 8, 128], BF16) for i in range(2)]
        eR1 = sb("eR", [128, 8, 128], BF16)
        eR = [eR1, eR1]
        Csb = [sb(f"Csb{i}", [128, 8, 128], BF16) for i in range(2)]
        xdtb = [sb(f"xdtb{i}", [128, 512], BF16) for i in range(2)]
        xdtdb = [sb(f"xdtdb{i}", [128, 512], BF16) for i in range(2)]
        BTb = [sb(f"BTb{i}", [128, 2, 128], BF16) for i in range(2)]
        sm = sb("sm", [128, 256])
        dD = sb("dD", [128, L, 4, 128])
        hT = sb("hT", [128, L, 2, 512])
        hTb = sb("hTb", [128, 2, 512], BF16)
        hcv = sb("hcv", [128, L, 2, 8, 3])
        hpl = sb("hpl", [128, L, 2, 4, 15])
        pTb = sb("pTb", [128, 2, TT], BF16)
        cst = sb("cst_sb", [128, 512 + 64 * (LAG + 1)])
        cstb = sb("cstb", [128, 384], BF16)
        prm = sb("prm_sb", [128, L * PL + 8])
        wdt = sb("wdt", [128, 2, 64], BF16)
        wpl = sb("wpl", [128, 2, 512], BF16)
        ring = [sb(f"ring{i}", [128, SLABW], BF16) for i in range(NRING)]
        PB = [pst(f"P{i}", [128, 512]) for i in range(7)]
        P7a = pst("P7a", [128, 512])
        P7b = PB[3][:, 0:128].bitcast(BF16)
        sems = [st.enter_context(nc.semaphore(f"s{i}")) for i in range(48)]
        block = st.enter_context(nc.Block())

        ident = cst[:, 0:128]
        tri = cst[:, 128:256]
        ones = cst[:, 256:384]
        negm = cst[:, 384:512]
        identb = cstb[:, 0:128]
        onesb = cstb[:, 256:384]

        def mm(out, lhsT, rhs, start, stop, r, w):
            P.add("pe", lambda e: e.matmul(out, lhsT, rhs, start=start, stop=stop), r, w)

        def tr(out, in_, idn, r, w):
            P.add("pe", lambda e: e.transpose(out, in_, idn), r, w)

        def act(out, in_, func, r, w, bias=None, scale=None):
            kw = {}
            if bias is not None:
                kw["bias"] = bias
            if scale is not None:
                kw["scale"] = scale
            P.add("act", lambda e: e.activation(out, in_, func, **kw), r, w)

        def tt(eng, out, a, b, op, r, w):
            P.add(eng, lambda e: e.tensor_tensor(out, a, b, op), r, w)

        def ts(eng, out, a, s1, s2, op0, op1, r, w):
            P.add(eng, lambda e: e.tensor_scalar(out, a, s1, s2, op0, op1), r, w)

        def ts1(eng, out, a, s1, op0, r, w):
            P.add(eng, lambda e: e.tensor_single_scalar(out, a, s1, op0), r, w)

        def stt(eng, out, in0, scalar, in1, op0, op1, r, w):
            P.add(eng, lambda e: e.scalar_tensor_tensor(out, in0, scalar, in1, op0, op1), r, w)

        def cp(eng, out, in_, r, w):
            if eng == "act":
                P.add("act", lambda e: e.activation(out, in_, AF.Copy), r, w)
            else:
                P.add(eng, lambda e: e.tensor_copy(out, in_), r, w)

        def dma(eng, out, in_, r, w, sem):
            P.add(eng, lambda e: e.dma_start(out=out, in_=in_), r, w, dma=sem)

        dma("pool", cst[:], cst_d, [], ["cst"], "su0")
        dma("pool", prm[:], prm_d, [], ["prm"], "su1")
        cp("dve", cstb[:], cst[:, 0:384], ["cst"], ["cstb"])
        for l in range(L):
            b0 = l * PL
            ts1("dve", prm[:, b0:b0 + 40], prm[:, b0:b0 + 40], 32.0, ALU.mult, ["prm"], ["prm"])
            ts1("dve", prm[:, b0 + 40:b0 + 44], prm[:, b0 + 40:b0 + 44], 16.0, ALU.mult, ["prm"], ["prm"])
            act(prm[:, b0 + 104:b0 + 112], prm[:, b0 + 104:b0 + 112], AF.Exp, ["prm"], ["prm"])
            ts1("dve", prm[:, b0 + 104:b0 + 112], prm[:, b0 + 104:b0 + 112], -1.0, ALU.mult, ["prm"], ["prm"])
        for l in range(L):
            for m in range(4):
                ts1("dve", dD[:, l, m, :], ident, prm[:, l * PL + 92 + m:l * PL + 93 + m], ALU.mult, ["cst", "prm"], ["dD"])
        sgrp = lambda s: 0 if s < 8 else (1 if s < 20 else 2)
        conv_done = set()

        def emit_conv(l, g):
            if l >= L or (l, g) in conv_done:
                return
            conv_done.add((l, g))
            s0, s1 = ((0, 8), (8, 20), (20, NSLAB))[g]
            src = wsl[l, SLAB_OFF[s0]:SLAB_OFF[s1]].rearrange("(a b) -> a b", a=16)
            dst = wbf[l, SLAB_OFF[s0]:SLAB_OFF[s1]].rearrange("(a b) -> a b", a=16)
            dma("pool", dst, src, [], [f"wbf{l}_{g}"], f"cv{l}_{g}")

        emit_conv(0, 0)

        import os
        _stop = os.environ.get("MK_STOP", "")

        _cnt = {}

        def stage(name):
            _cnt[name] = _cnt.get(name, 0) + 1
            if f"{name}:{_cnt[name]}" == _stop or name == _stop:
                P.enabled = False

        stage("setup")
        ring_ctr = [0]

        def load_slab(l, s):
            kc, ncol = SLAB_SH[s]
            n = kc * ncol
            slot = ring_ctr[0] % NRING
            ring_ctr[0] += 1
            dma("sp", ring[slot][:, 0:n], wslab(l, s), [f"wbf{l}_{sgrp(s)}"], [f"ring{slot}"], f"rg{slot}")
            view = ring[slot][:, 0:n].rearrange("p (k c) -> p k c", k=kc)
            return view, f"ring{slot}"

        bank_ctr = [0]

        def next_bank():
            b = (0, 1, 2, 3, 5, 6)[bank_ctr[0] % 6]
            bank_ctr[0] += 1
            return PB[b], f"ps{b}"

        def pcol(l, off, m=None):
            c = l * PL + off + (0 if m is None else m)
            return prm[:, c:c + 1]

        sq_ctr = [0]

        def stats_rstd(src_chunks, src_keys, nd, rs_idx, statbank, statkey, ntok, sq_eng="act"):
            n = len(src_chunks)
            for i, (ap, key) in enumerate(zip(src_chunks, src_keys)):
                slot = sq_ctr[0] % 2
                sq_ctr[0] += 1
                if sq_eng == "act":
                    act(sqb[:, slot, 0:ntok], ap, AF.Square, [key], [f"sqb{slot}"])
                else:
                    tt(sq_eng, sqb[:, slot, 0:ntok], ap, ap, ALU.mult, [key], [f"sqb{slot}"])
                mm(statbank[:, 0:ntok], onesb, sqb[:, slot, 0:ntok], i == 0, i == n - 1,
                   [f"sqb{slot}", "cstb"], [statkey])
            act(rs[:, rs_idx, 0:ntok], statbank[:, 0:ntok], AF.Ln, [statkey], [f"rs{rs_idx}"], bias=float(nd * EPS))
            act(rs[:, rs_idx, 0:ntok], rs[:, rs_idx, 0:ntok], AF.Exp, [f"rs{rs_idx}"], [f"rs{rs_idx}"], scale=-0.5)

        def prenorm(l, goff, ntok):
            stats_rstd([h[:, c, 0:ntok] for c in range(8)], [f"h{c}" for c in range(8)], 1024, 0, PB[4], "ps4", ntok)
            for c in range(8):
                stt("dve", hn[:, c, 0:ntok], h[:, c, 0:ntok], pcol(l, goff, c), rs[:, 0, 0:ntok], ALU.mult, ALU.mult,
                    [f"h{c}", "rs0", "prm"], [f"hn{c}"])

        def mix_square(m, ntok):
            slot = sq_ctr[0] % 2
            sq_ctr[0] += 1
            act(sqb[:, slot, 0:ntok], mix[:, m, 0:ntok], AF.Square, [f"mix{m}"], [f"sqb{slot}"])
            return (m, slot)

        def mix_stat(pend, ntok):
            m, slot = pend
            mm(PB[4][:, 0:ntok], onesb, sqb[:, slot, 0:ntok], m == 0, m == 7, [f"sqb{slot}", "cstb"], ["ps4"])

        def postnorm_residual(l, goff, ntok, cast_hn=False):
            act(rs[:, 0, 0:ntok], PB[4][:, 0:ntok], AF.Ln, ["ps4"], ["rs0"], bias=float(1024 * EPS))
            act(rs[:, 0, 0:ntok], rs[:, 0, 0:ntok], AF.Exp, ["rs0"], ["rs0"], scale=-0.5)
            for c in range(8):
                stt("dve", mix[:, c, 0:ntok], mix[:, c, 0:ntok], pcol(l, goff, c), rs[:, 0, 0:ntok], ALU.mult, ALU.mult,
                    [f"mix{c}", "rs0", "prm"], [f"mix{c}"])
                tt("dve", h[:, c, 0:ntok], h[:, c, 0:ntok], mix[:, c, 0:ntok], ALU.add, [f"h{c}", f"mix{c}"], [f"h{c}"])
                if cast_hn:
                    cp("act", hn[:, c, 0:ntok], h[:, c, 0:ntok], [f"h{c}"], [f"hn{c}"])

        def layer(l, segs, ntok, pT_src, fix_col, snap, out_targets, post_ffn=None):
            dma("pool", pTb[:, :, 0:ntok], pT_src.rearrange("(c p) t -> p c t", p=128), [], ["pTb"], "pt")
            par = l % 2
            dma("sp", wdt[:, par, :], wslab(l, 3), [f"wbf{l}_0"], [f"wdt{par}"], f"wd{par}")
            dma("sp", wpl[:, par, :], wslab(l, 5), [f"wbf{l}_0"], [f"wpl{par}"], f"wp{par}")
            for (slot, sl, Q, c0) in segs:
                si = segs.index((slot, sl, Q, c0))
                eo = si * (3 + sl)
                uo = si * (15 + sl)
                cp("pool", xbc[:, :, eo:eo + 3], hcv[:, l, slot], [f"hcv{l}_{slot}"], [f"xbch{si}"])
                cp("pool", uex[:, :, uo:uo + 15], hpl[:, l, slot], [f"hpl{l}_{slot}"], [f"uexh{si}"])
            stage("hist")
            emit_conv(l, 1)
            prenorm(l, 0, ntok)
            stage("norm1")
            hn_keys = [f"hn{c}" for c in range(8)]

            def proj_chunk(slabv, slabk, cidx, src, src_keys, nk):
                ps, pk = next_bank()
                for kc in range(nk):
                    mm(ps[:, 0:ntok], slabv[:, kc, cidx * 128:(cidx + 1) * 128], src[:, kc, 0:ntok], kc == 0, kc == nk - 1,
                       [slabk, src_keys[kc]], [pk])
                return ps, pk

            chunks = []
            for si, (slot, sl, Q, c0) in enumerate(segs):
                for c in range(sl // Q):
                    chunks.append((len(chunks), si, slot, Q, c0 + c * Q, c))
            nch = len(chunks)
            Q = segs[0][2]
            W8 = nch * 8
            for (gi, si, slot, Q_, col, c) in chunks:
                for kc in range(8):
                    mm(P7a[0:Q, gi * 8:(gi + 1) * 8], hn[:, kc, col:col + Q], wdt[:, par, kc * 8:(kc + 1) * 8], kc == 0, kc == 7,
                       [f"hn{kc}", f"wdt{par}"], ["ps7"])
            v3 = lambda ap: ap.rearrange("p (c h) -> p c h", h=8)
            bc3 = lambda ap: ap.unsqueeze(1).to_broadcast([Q, nch, 8])
            tt("dve", v3(sm[0:Q, 0:W8]), v3(P7a[0:Q, 0:W8]), bc3(prm[0:Q, l * PL + 96:l * PL + 104]), ALU.add, ["ps7", "prm"], ["sm_a"])
            act(sm[0:Q, 0:W8], sm[0:Q, 0:W8], AF.Exp, ["sm_a"], ["sm_a"])
            act(sm[0:Q, 32:32 + W8], sm[0:Q, 0:W8], AF.Ln, ["sm_a"], ["sm_dt"], bias=1.0)
            tt("dve", v3(sm[0:Q, 64:64 + W8]), v3(sm[0:Q, 32:32 + W8]), bc3(prm[0:Q, l * PL + 104:l * PL + 112]), ALU.mult,
               ["sm_dt", "prm"], ["sm_dta"])
            mm(P7a[0:Q, 32:32 + W8], tri[0:Q, 0:Q], sm[0:Q, 64:64 + W8], True, True, ["cst", "sm_dta"], ["ps7"])
            mm(P7a[:, 64:64 + W8], ones[0:Q, 0:128], sm[0:Q, 64:64 + W8], True, True, ["cst", "sm_dta"], ["ps7"])
            cp("dve", sm[0:Q, 96:96 + W8], P7a[0:Q, 32:32 + W8], ["ps7"], ["sm_ac"])
            tt("dve", sm[0:Q, 128:128 + W8], P7a[0:Q, 64:64 + W8], sm[0:Q, 96:96 + W8], ALU.subtract, ["ps7", "sm_ac"], ["sm_dd"])
            act(sm[0:Q, 160:160 + W8], sm[0:Q, 128:128 + W8], AF.Exp, ["sm_dd"], ["sm_de"])
            act(sm[:, 192:192 + W8], P7a[:, 64:64 + W8], AF.Exp, ["ps7"], ["sm_dec"])

            for sidx, mbase in ((1, 0), (2, 4)):
                sv, sk = load_slab(l, sidx)
                for ci in range(4):
                    m = mbase + ci
                    ps, pk = proj_chunk(sv, sk, ci, hn, hn_keys, 8)
                    for si, (slot, sl, Q_, c0) in enumerate(segs):
                        eo = si * (3 + sl)
                        cp("act", xbc[:, m, eo + 3:eo + 3 + sl], ps[:, c0:c0 + sl], [pk], [f"xbc{m}_{si}"])
            stage("inproj")
            for m in (0, 4, 1, 5, 2, 6, 3, 7):
                for si, (slot, sl, Q_, c0) in enumerate(segs):
                    eo = si * (3 + sl)
                    a = (tA, tB, tC, tD)[(m % 2) + 2 * (m // 4)][:, c0:c0 + sl]
                    ak = ("tA", "tB", "tC", "tD")[(m % 2) + 2 * (m // 4)]
                    rk = [f"xbc{m}_{si}", f"xbch{si}", "prm"]
                    ce = "dve"
                    ts(ce, a, xbc[:, m, eo:eo + sl], pcol(l, 52, m * 4 + 0), pcol(l, 44, m), ALU.mult, ALU.add, rk, [ak])
                    for k in range(1, 4):
                        stt(ce, a, xbc[:, m, eo + k:eo + k + sl], pcol(l, 52, m * 4 + k), a, ALU.mult, ALU.add, rk + [ak], [ak])
                    if m < 4:
                        act(xs[:, m, c0:c0 + sl], a, AF.Silu, [ak], [f"xs{m}_{si}_{c}" for c in range(sl // Q_)])
                    else:
                        act(BCb[:, m - 4, c0:c0 + sl], a, AF.Silu, [ak], [f"BC{m - 4}_{si}"])
            sv, sk = load_slab(l, 0)
            for ci in range(4):
                ps, pk = proj_chunk(sv, sk, ci, hn, hn_keys, 8)
                act(zs[:, ci, 0:ntok], ps[:, 0:ntok], AF.Silu, [pk], [f"zs{ci}"])
            sv, sk = load_slab(l, 4)
            for ci in range(4):
                ps, pk = proj_chunk(sv, sk, ci, hn, hn_keys, 8)
                for si, (slot, sl, Q_, c0) in enumerate(segs):
                    uo = si * (15 + sl)
                    cp("act", uex[:, ci, uo + 15:uo + 15 + sl], ps[:, c0:c0 + sl], [pk], [f"uex{ci}_{si}"])
            for si, (slot, sl, Q_, c0) in enumerate(segs):
                eo = si * (3 + sl)
                cp("pool", hcv[:, l, slot], xbc[:, :, eo + sl:eo + sl + 3],
                   [f"xbc{m}_{si}" for m in range(8)] + [f"xbch{si}"], [f"hcv{l}_{slot}"])
                if snap is not None:
                    dma("pool", out_targets["conv"][si](snap, l), hcv[:, l, slot], [f"hcv{l}_{slot}"], [], "oc")
            stage("conv")
            for m in (2, 0, 3, 1):
                w = POOL_W[m]
                for si, (slot, sl, Q_, c0) in enumerate(segs):
                    uo = si * (15 + sl)
                    E = 15 + sl
                    u = uex[:, m, uo:uo + E]
                    uk = [f"uex{m}_{si}", f"uexh{si}"]
                    src, srck = u, uk
                    pe_ = "dve" if m < 2 else "pool"
                    bufs = [(tA, "tA"), (tB, "tB")] if m < 2 else [(tC, "tC"), (tD, "tD")]
                    step = 1
                    lo = 0
                    bi = 0
                    while step < w:
                        lo = lo + step
                        dst, dk = bufs[bi]
                        tt(pe_, dst[:, lo:E], src[:, lo:E], src[:, lo - step:E - step], ALU.add, srck, [dk])
                        src, srck = dst, [dk]
                        bi ^= 1
                        step *= 2
                    stt("dve", hn[:, 4 + m, c0:c0 + sl], src[:, 15:E], 1.0 / w, u[:, 15:E], ALU.mult, ALU.subtract,
                        srck + uk, [f"hn{4 + m}"])
                    if fix_col is not None:
                        tt("dve", sm[:, 224:240], src[:, 15:31], cst[:, fix_col + m * 16:fix_col + (m + 1) * 16], ALU.mult,
                           srck + ["cst"], ["smfix"])
                        tt("dve", hn[:, 4 + m, c0:c0 + 16], sm[:, 224:240], u[:, 15:31], ALU.subtract,
                           ["smfix"] + uk, [f"hn{4 + m}"])
                ps, pk = next_bank()
                mm(ps[:, 0:ntok], wpl[:, par, m * 128:(m + 1) * 128], hn[:, 4 + m, 0:ntok], True, True,
                   [f"wpl{par}", f"hn{4 + m}"], [pk])
                ts("dve", hn[:, 4 + m, 0:ntok], ps[:, 0:ntok], pcol(l, 84, m), pcol(l, 88, m), ALU.add, ALU.mult,
                   [pk, "prm"], [f"hn{4 + m}"])
            for si, (slot, sl, Q_, c0) in enumerate(segs):
                uo = si * (15 + sl)
                cp("pool", hpl[:, l, slot], uex[:, :, uo + sl:uo + sl + 15],
                   [f"uex{m}_{si}" for m in range(4)] + [f"uexh{si}"], [f"hpl{l}_{slot}"])
                if snap is not None:
                    dma("pool", out_targets["pool"][si](snap, l), hpl[:, l, slot], [f"hpl{l}_{slot}"], [], "op")
            stage("pool")
            hb = 512 // Q
            nb = 8 // hb

            def ssd_front(gi, si, slot, Q_, col, c):
                p = gi % 2
                cs = slice(col, col + Q)
                dta = sm[0:Q, 64 + gi * 8:64 + gi * 8 + 8]
                acs = sm[0:Q, 96 + gi * 8:96 + gi * 8 + 8]
                bA, LT, eRp, Cs = bufA[p], LTb[p], eR[p], Csb[p]
                Rv = []
                for b in range(nb):
                    bank = PB[5 + b]
                    for hh in range(hb * b, hb * (b + 1)):
                        mm(bank[:, (hh - hb * b) * Q:(hh - hb * b + 1) * Q], dta[:, hh:hh + 1].to_broadcast([Q, 128]), tri[0:Q, 0:Q],
                           True, True, ["cst", "sm_dta"], [f"ps{5 + b}"])
                    Rv.append((bank[:, 0:hb * Q].rearrange("p (h q) -> p h q", q=Q), f"ps{5 + b}"))
                for b, (R, Rk) in enumerate(Rv):
                    hsl = slice(hb * b, hb * (b + 1))
                    for hh in range(hb * b, hb * (b + 1)):
                        stt("dve", bA[0:Q, hh, 0:Q], R[0:Q, hh - hb * b, :], acs[:, hh:hh + 1], negm[0:Q, 0:Q], ALU.subtract, ALU.add,
                            [Rk, "sm_ac", "cst"], [f"bufA{p}_{b}"])
                    act(LT[0:Q, hsl, 0:Q], bA[0:Q, hsl, 0:Q], AF.Exp, [f"bufA{p}_{b}"], [f"LT{p}_{b}"])
                    act(eRp[:, hsl, 0:Q], R, AF.Exp, [Rk], [f"eR_{b}"])
                LTk = [f"LT{p}_{b}" for b in range(nb)]
                eRk = [f"eR_{b}" for b in range(nb)]
                for g in range(2):
                    mm(P7a[0:Q, 128 + g * Q:128 + (g + 1) * Q], BCb[:, g, cs], BCb[:, 2 + g, cs], True, True,
                       [f"BC{g}_{si}", f"BC{2 + g}_{si}"], ["ps7"])
                for g in range(2):
                    tt("dve", LT[0:Q, 4 * g:4 * g + 4, 0:Q], LT[0:Q, 4 * g:4 * g + 4, 0:Q],
                       P7a[0:Q, 128 + g * Q:128 + (g + 1) * Q].unsqueeze(1).to_broadcast([Q, 4, Q]), ALU.mult,
                       LTk + ["ps7"], LTk)
                for g in range(2):
                    tt("pool", Cs[:, 4 * g:4 * g + 4, 0:Q], eRp[:, 4 * g:4 * g + 4, 0:Q],
                       BCb[:, 2 + g, cs].unsqueeze(1).to_broadcast([128, 4, Q]), ALU.mult,
                       eRk + [f"BC{2 + g}_{si}"], [f"Csb{p}"])
                for m in range(4):
                    tr(PB[4][0:Q, m * 128:(m + 1) * 128], xs[:, m, cs], ident, [f"xs{m}_{si}_{c}", "cst"], ["ps4"])
                tt("dve", xdtb[p][0:Q, :].rearrange("p (h d) -> p h d", d=64),
                   PB[4][0:Q, :].rearrange("p (h d) -> p h d", d=64),
                   sm[0:Q, 32 + gi * 8:32 + gi * 8 + 8].unsqueeze(2).to_broadcast([Q, 8, 64]), ALU.mult, ["ps4", "sm_dt"], [f"xdtb{p}"])
                tt("pool", xdtdb[p][0:Q, :].rearrange("p (h d) -> p h d", d=64),
                   xdtb[p][0:Q, :].rearrange("p (h d) -> p h d", d=64),
                   sm[0:Q, 160 + gi * 8:160 + gi * 8 + 8].unsqueeze(2).to_broadcast([Q, 8, 64]), ALU.mult,
                   [f"xdtb{p}", "sm_de"], [f"xdtdb{p}"])
                for g in range(2):
                    tr(P7b[0:Q, g * 128:(g + 1) * 128], BCb[:, g, cs], identb, [f"BC{g}_{si}", "cstb"], ["ps3"])
                cp("act", BTb[p][0:Q, :, :], P7b[0:Q, :].rearrange("p (g n) -> p g n", g=2), ["ps3"], [f"BTb{p}"])

            def ssd_back(gi, si, slot, Q_, col, c, last_in_seg):
                p = gi % 2
                cs = slice(col, col + Q)
                LT, Cs = LTb[p], Csb[p]
                LTk = [f"LT{p}_{b}" for b in range(nb)]
                for hh in range(8):
                    pr = hh // 2
                    if Q == 128:
                        yp = PB[hh // 4][:, (hh % 4) * Q:(hh % 4 + 1) * Q]
                        yk = f"ps{hh // 4}"
                    else:
                        yp = PB[0][:, hh * Q:(hh + 1) * Q]
                        yk = "ps0"
                    mm(yp, xdtb[p][0:Q, pr * 128:(pr + 1) * 128], LT[0:Q, hh, 0:Q], True, False, [f"xdtb{p}"] + LTk, [yk])
                    mm(yp, hTb[:, slot, pr * 128:(pr + 1) * 128], Cs[:, hh, 0:Q], False, False, [f"hTb{slot}", f"Csb{p}"], [yk])
                    mm(yp, dD[:, l, pr, :], xs[:, pr, cs], False, True, ["dD", f"xs{pr}_{si}_{c}"], [yk])
                nbk = 2 if Q == 128 else 1
                ppb = 4 // nbk
                for bk in range(nbk):
                    v = PB[bk][:, :].rearrange("p (a two q) -> p a two q", two=2, q=Q)
                    keys = [f"xs{pr}_{si}_{c}" for pr in range(bk * ppb, (bk + 1) * ppb)]
                    zkeys = [f"zs{pr}" for pr in range(bk * ppb, (bk + 1) * ppb)]
                    for par_ in range(2):
                        po = 64 * par_
                        tt("dve", zs[po:po + 64, bk * ppb:(bk + 1) * ppb, cs], v[po:po + 64, :, par_, :],
                           zs[po:po + 64, bk * ppb:(bk + 1) * ppb, cs], ALU.mult, [f"ps{bk}"] + zkeys, zkeys)
                for g in range(2):
                    mm(PB[2][:, g * 256:(g + 1) * 256], BTb[p][0:Q, g, :], xdtdb[p][0:Q, g * 256:(g + 1) * 256], True, True,
                       [f"BTb{p}", f"xdtdb{p}"], ["ps2"])
                hv = hT[:, l, slot, :].rearrange("p (h d) -> p h d", d=64)
                tt("dve", hv, hv, sm[:, 192 + gi * 8:192 + gi * 8 + 8].unsqueeze(2).to_broadcast([128, 8, 64]), ALU.mult,
                   [f"hT{l}_{slot}", "sm_dec"], [f"hT{l}_{slot}"])
                tt("dve", hT[:, l, slot, :], hT[:, l, slot, :], PB[2][:, :], ALU.add, [f"hT{l}_{slot}", "ps2"], [f"hT{l}_{slot}"])
                if not last_in_seg:
                    cp("act", hTb[:, slot, :], hT[:, l, slot, :], [f"hT{l}_{slot}"], [f"hTb{slot}"])
                elif snap is not None:
                    dma("pool", out_targets["ssm"][si](snap, l), hT[:, l, slot, :], [f"hT{l}_{slot}"], [], "os")

            for si, (slot, sl, Q_, c0) in enumerate(segs):
                cp("act", hTb[:, slot, :], hT[:, l, slot, :], [f"hT{l}_{slot}"], [f"hTb{slot}"])
            ssd_front(*chunks[0])
            for i, ch in enumerate(chunks):
                if i + 1 < nch:
                    ssd_front(*chunks[i + 1])
                last_in_seg = (i + 1 == nch) or (chunks[i + 1][1] != ch[1])
                ssd_back(*ch, last_in_seg)
            stage("ssd")
            xs_keys = lambda m: [f"xs{m}_{si}_{c}" for si, sg in enumerate(segs) for c in range(sg[1] // sg[2])]
            for g in range(2):
                stats_rstd([zs[:, 2 * g + i, 0:ntok] for i in range(2)], [f"zs{2 * g + i}" for i in range(2)], 256, g,
                           PB[4 + g], f"ps{4 + g}", ntok)
            for m in range(4):
                stt("dve", hn[:, m, 0:ntok], zs[:, m, 0:ntok], pcol(l, 40, m), rs[:, m // 2, 0:ntok], ALU.mult, ALU.mult,
                    [f"zs{m}", f"rs{m // 2}", "prm"], [f"hn{m}"])
            mk = [f"hn{c}" for c in range(8)]
            pend = None
            for sidx in (6, 7):
                sv, sk = load_slab(l, sidx)
                for ci in range(4):
                    m = (sidx - 6) * 4 + ci
                    ps, pk = proj_chunk(sv, sk, ci, hn, mk, 8)
                    if pend is not None:
                        mix_stat(pend, ntok)
                    cp("act", mix[:, m, 0:ntok], ps[:, 0:ntok], [pk], [f"mix{m}"])
                    pend = mix_square(m, ntok)
            mix_stat(pend, ntok)
            postnorm_residual(l, 8, ntok)
            stage("mixer")
            emit_conv(l, 2)
            emit_conv(l + 1, 0)
            prenorm(l, 16, ntok)
            for j in range(6):
                gv, gk = load_slab(l, 8 + 2 * j)
                uv, uk_ = load_slab(l, 9 + 2 * j)
                for ci in range(4 if j < 5 else 2):
                    m = 4 * j + ci
                    psg, pkg = proj_chunk(gv, gk, ci, hn, hn_keys, 8)
                    psu, pku = proj_chunk(uv, uk_, ci, hn, hn_keys, 8)
                    sgb, sgk = ((tA, "tA"), (tB, "tB"))[m % 2]
                    act(sgb[:, 0:ntok], psg[:, 0:ntok], AF.Silu, [pkg], [sgk])
                    tt("dve", actb[:, m, 0:ntok], sgb[:, 0:ntok], psu[:, 0:ntok], ALU.mult, [sgk, pku], [f"actb{m}"])
            ak_ = [f"actb{m}" for m in range(22)]
            pend = None
            for m in range(8):
                dv, dk = load_slab(l, 20 + m)
                ps, pk = next_bank()
                for kc in range(22):
                    mm(ps[:, 0:ntok], dv[:, kc, :], actb[:, kc, 0:ntok], kc == 0, kc == 21, [dk, ak_[kc]], [pk])
                if pend is not None:
                    mix_stat(pend, ntok)
                cp("act", mix[:, m, 0:ntok], ps[:, 0:ntok], [pk], [f"mix{m}"])
                pend = mix_square(m, ntok)
            mix_stat(pend, ntok)
            postnorm_residual(l, 24, ntok, cast_hn=True)
            if post_ffn is not None:
                post_ffn()
            stage("ffn")
            ppv, ppk = load_slab(l, 30)
            pend = None
            for sidx in (28, 29):
                sv, sk = load_slab(l, sidx)
                for ci in range(4):
                    m = (sidx - 28) * 4 + ci
                    psg, pkg = proj_chunk(sv, sk, ci, hn, hn_keys, 8)
                    pse, pke = next_bank()
                    for kc in range(2):
                        mm(pse[:, 0:ntok], ppv[:, kc, m * 128:(m + 1) * 128], pTb[:, kc, 0:ntok], kc == 0, kc == 1, [ppk, "pTb"], [pke])
                    sgb, sgk = ((tA, "tA"), (tB, "tB"))[m % 2]
                    act(sgb[:, 0:ntok], psg[:, 0:ntok], AF.Sigmoid, [pkg], [sgk])
                    tt("dve", mix[:, m, 0:ntok], sgb[:, 0:ntok], pse[:, 0:ntok], ALU.mult, [sgk, pke], [f"mix{m}"])
            for m in range(8):
                mix_stat(mix_square(m, ntok), ntok)
            postnorm_residual(l, 32, ntok)
            stage("ple")

        hkeys = [f"h{c}" for c in range(8)]

        fB = prm[:, L * PL + 0:L * PL + 1]
        keep = prm[:, L * PL + 1:L * PL + 2]

        def run_tile(x_src, g_src, y_dst, ntok, segs, pT_of_layer, fix_col, snap, out_targets):
            dma("pool", h[:, :, 0:ntok], x_src.rearrange("(c p) t -> p c t", p=128), [], hkeys, "xin")
            if g_src is not None:
                gv, gk = g_src
                dma("pool", mix[:, :, 0:ntok], gv.rearrange("(c p) t -> p c t", p=128), [gk], [f"mix{c}" for c in range(8)], "gin")
                for c in range(8):
                    stt("dve", h[:, c, 0:ntok], mix[:, c, 0:ntok], fB, h[:, c, 0:ntok], ALU.mult, ALU.add,
                        [f"mix{c}", f"h{c}", "prm"], [f"h{c}"])
            for l in range(L):
                layer(l, segs, ntok, pT_of_layer(l), fix_col, snap, out_targets)
            dma("pool", y_dst.rearrange("(c p) t -> p c t", p=128), h[:, :, 0:ntok], hkeys, [], "yout")

        xst = actb[:, 0:16, :].bitcast(F32).rearrange("p (a two) n -> p a (two n)", two=2)
        xst_keys = [f"actb{m}" for m in range(16)]

        def prefetch_x(k):
            cols = slice(k * TT, (k + 1) * TT)
            dma("pool", xst, xT[:, cols].rearrange("(c p) t -> p c t", p=128), [], xst_keys, "xin")

        def exchange(x_t, g_t, ntok, xk, gk):
            dma("pool", x_t.ap().rearrange("(c p) t -> p c t", p=128), h[:, :, 0:ntok], hkeys + [gk], [xk], "xout")
            P.add("pool", lambda e: e.collective_compute("AllGather", ALU.bypass, replica_groups=PAIRS,
                                                          ins=[x_t.ap().opt()], outs=[g_t.ap().opt()]),
                  [xk], [gk], dma="cc", inc=1)

        segs = [(0, 64, 64, 0), (1, 64, 64, 64)]
        tg = {
            "ssm": [lambda sn, l, b=b: ssm_s[sn, l, b] for b in range(2)],
            "conv": [lambda sn, l, b=b: conv_s[sn, l, b] for b in range(2)],
            "pool": [lambda sn, l, b=b: pool_s[sn, l, b] for b in range(2)],
        }
        for sstep in range(2):
            for l in range(L):
                for b in range(2):
                    dma("pool", hT[:, l, b, :], hs0[l, b], [], [f"hT{l}_{b}"], "st")
                    dma("pool", hcv[:, l, b], cv0[l, b], [], [f"hcv{l}_{b}"], "st")
                    dma("pool", hpl[:, l, b], pl0[l, b], [], [f"hpl{l}_{b}"], "st")
            run_tile(xsT, None if sstep == 0 else (gs_t.ap()[0:1024, :], "gs"), ysT[sstep], 128, segs,
                     lambda l: psT[l], None, sstep, tg)
            if sstep == 0:
                exchange(sx_t, gs_t, 128, "sx", "gs")
        for l in range(L):
            P.add("dve", lambda e, l=l: e.memset(hT[:, l, 0, :], 0.0), [], [f"hT{l}_0"])
            P.add("dve", lambda e, l=l: e.memset(hcv[:, l, 0], 0.0), [], [f"hcv{l}_0"])
            P.add("dve", lambda e, l=l: e.memset(hpl[:, l, 0], 0.0), [], [f"hpl{l}_0"])
        tg = {
            "ssm": [lambda sn, l: ssm_p[sn, l]],
            "conv": [lambda sn, l: conv_p[sn, l]],
            "pool": [lambda sn, l: pool_p[sn, l]],
        }
        NS = n_ptiles + LAG
        seg_p = [(0, TT, 128, 0)]
        prefetch_x(0)
        for k in range(NS):
            cols = slice(k * TT, (k + 1) * TT)
            snap = {n_ptiles - 1: 0, NS - 1: 1}.get(k)
            fix_col = 512 + 64 * k if k <= LAG else None
            for c in range(8):
                if k >= LAG:
                    stt("dve", h[:, c, :], mix[:, c, :], fB, xst[:, c, :], ALU.mult, ALU.add,
                        [f"mix{c}", "prm"] + xst_keys, [f"h{c}"])
                else:
                    cp("dve", h[:, c, :], xst[:, c, :], xst_keys, [f"h{c}"])
            for l in range(L):
                pf = (lambda k=k: prefetch_x(k + 1)) if (l == L - 1 and k + 1 < NS) else None
                layer(l, seg_p, TT, ppT[l][:, cols], fix_col, snap, tg, post_ffn=pf)
            if k == LAG - 1:
                for l in range(L):
                    ts1("dve", hT[:, l, 0, :], hT[:, l, 0, :], keep, ALU.mult, [f"hT{l}_0", "prm"], [f"hT{l}_0"])
                    ts1("dve", hcv[:, l, 0], hcv[:, l, 0], keep, ALU.mult, [f"hcv{l}_0", "prm"], [f"hcv{l}_0"])
                    ts1("dve", hpl[:, l, 0], hpl[:, l, 0], keep, ALU.mult, [f"hpl{l}_0", "prm"], [f"hpl{l}_0"])
            px, pk_ = px_t[k % 2], f"px{k % 2}"
            dma("pool", px.ap().rearrange("(c p) t -> p c t", p=128), h[:, :, :], hkeys, [pk_], "xout")
            if k + 1 >= LAG and k + 1 < NS:
                gn = gp_t[(k + 1) % 2]
                dma("pool", mix[:, :, :], gn.ap()[0:1024, :].rearrange("(c p) t -> p c t", p=128), [f"gp{(k + 1) % 2}"],
                    [f"mix{c}" for c in range(8)], "gin")
            dma("pool", yT[:, cols], px.ap(), [pk_], [], "yout")
            if k < n_ptiles:
                P.add("pool", lambda e, px=px, k=k: e.collective_compute("AllGather", ALU.bypass, replica_groups=PAIRS,
                                                                         ins=[px.ap().opt()], outs=[gp_t[k % 2].ap().opt()]),
                      [pk_], [f"gp{k % 2}"], dma="cc", inc=1)
        P.emit(nc, block, sems)
    return nc


def _slab(W, c0, ncol):
    K = W.shape[0]
    kc = K // 128
    a = W[:, c0:c0 + ncol].reshape(kc, 128, ncol).transpose(1, 0, 2).reshape(128, kc * ncol)
    out = np.zeros((128, SLABW), np.float32)
    out[:, :kc * ncol] = a
    return out


def make_slabs(inp, layers):
    L = len(layers)
    wsl4 = np.zeros((L, NSLAB, 128, SLABW), np.float32)
    for li, l in enumerate(layers):
        w_in = np.asarray(inp["w_in"][l])
        wsl4[li, 0] = _slab(w_in, 0, 512)
        wsl4[li, 1] = _slab(w_in, 512, 512)
        wsl4[li, 2] = _slab(w_in, 1024, 512)
        wsl4[li, 3] = _slab(w_in, 1536, 8)
        wsl4[li, 4] = _slab(w_in, 1544, 512)
        pw = np.asarray(inp["pool_w"][l])
        wsl4[li, 5, :, :512] = pw.transpose(1, 0, 2).reshape(128, 512)
        w_out = np.asarray(inp["w_out"][l])
        wsl4[li, 6] = _slab(w_out, 0, 512)
        wsl4[li, 7] = _slab(w_out, 512, 512)
        wg = np.asarray(inp["w_gate"][l])
        wu = np.asarray(inp["w_up"][l])
        for j in range(6):
            ncol = 512 if j < 5 else 256
            wsl4[li, 8 + 2 * j] = _slab(wg, j * 512, ncol)
            wsl4[li, 9 + 2 * j] = _slab(wu, j * 512, ncol)
        wd = np.asarray(inp["w_down"][l])
        for m in range(8):
            wsl4[li, 20 + m] = _slab(wd, m * 128, 128)
        wpg = np.asarray(inp["w_ple_gate"][l])
        wsl4[li, 28] = _slab(wpg, 0, 512)
        wsl4[li, 29] = _slab(wpg, 512, 512)
        wsl4[li, 30] = _slab(np.asarray(inp["w_ple_proj"][l]), 0, 1024)
    wsl = np.zeros((L, WTOT), np.float32)
    for li in range(L):
        for s_ in range(NSLAB):
            wsl[li, SLAB_OFF[s_]:SLAB_OFF[s_ + 1]] = wsl4[li, s_, :, :SLAB_N[s_]].reshape(-1)
    return wsl


def make_params(inp, layers, role):
    L = len(layers)
    prm = np.zeros((128, L * PL + 8), np.float32)
    fm = lambda v: np.asarray(v).reshape(-1, 128).T
    for li, l in enumerate(layers):
        b = li * PL
        prm[:, b + 0:b + 8] = fm(inp["pre_mix_g"][l])
        prm[:, b + 8:b + 16] = fm(inp["post_mix_g"][l])
        prm[:, b + 16:b + 24] = fm(inp["pre_ffn_g"][l])
        prm[:, b + 24:b + 32] = fm(inp["post_ffn_g"][l])
        prm[:, b + 32:b + 40] = fm(inp["ple_norm_g"][l])
        prm[:, b + 40:b + 44] = fm(inp["ssm_norm_g"][l])
        prm[:, b + 44:b + 52] = fm(inp["conv_b"][l])
        cw = np.asarray(inp["conv_w"][l])
        for m in range(8):
            for k in range(4):
                prm[:, b + 52 + m * 4 + k] = cw[k, m * 128:(m + 1) * 128]
        prm[:, b + 84:b + 88] = fm(np.asarray(inp["pool_b"][l]).reshape(-1))
        prm[:, b + 88:b + 92] = fm(inp["pool_scale"][l])
        dsk = np.asarray(inp["d_skip"][l])
        prm[:, b + 92:b + 96] = np.repeat(dsk, 64).reshape(4, 128).T
        prm[:, b + 96:b + 104] = np.broadcast_to(np.asarray(inp["dt_bias"][l])[None, :], (128, 8))
        prm[:, b + 104:b + 112] = np.broadcast_to(np.asarray(inp["a_log"][l])[None, :], (128, 8))
    prm[:, L * PL + 0] = 1.0 if role == 1 else 0.0
    prm[:, L * PL + 1] = 0.0 if role == 1 else 1.0
    return prm


def make_consts(role):
    c = np.zeros((128, 512 + 64 * (LAG + 1)), np.float32)
    c[:, 0:128] = np.eye(128, dtype=np.float32)
    c[:, 128:256] = np.triu(np.ones((128, 128), np.float32))
    c[:, 256:384] = 1.0
    c[:, 384:512] = np.where(np.arange(128)[None, :] >= np.arange(128)[:, None], 0.0, -30000.0)
    for step in range(LAG + 1):
        for m, w in enumerate(POOL_W):
            if step == role * LAG:
                v = 1.0 / np.minimum(np.arange(16) + 1, w)
            else:
                v = np.full(16, 1.0 / w)
            c[:, 512 + step * 64 + m * 16:512 + step * 64 + (m + 1) * 16] = v[None, :]
    return c


def prep_core_inputs(inp, layers, role, prompt_b, n_ptiles, sample_bs, shared):
    d = dict(shared)
    L = len(layers)
    ntok = n_ptiles * TT
    NPT = (n_ptiles + LAG) * TT
    xT = np.zeros((1024, NPT), np.float32)
    xsT = np.zeros((1024, 128), np.float32)
    ppT = np.zeros((L, 256, NPT), np.float32)
    off = 0 if role == 0 else LAG * TT
    if role == 0:
        xT[:, 0:ntok] = np.asarray(inp["x_prompt"][prompt_b, 0:ntok]).T
        xsT[:] = np.concatenate([np.asarray(inp["x_sample"][b]) for b in sample_bs], 0).T
    ppT[:, :, off:off + ntok] = np.asarray(inp["p_prompt"])[layers][:, prompt_b, 0:ntok].transpose(0, 2, 1)
    d["xT"] = xT
    d["xsT"] = xsT
    d["ppT"] = ppT
    d["psT"] = np.ascontiguousarray(
        np.concatenate([np.asarray(inp["p_sample"])[layers][:, b] for b in sample_bs], 1).transpose(0, 2, 1))
    ss = np.asarray(inp["state_ssm"])[layers][:, sample_bs]
    d["hs0"] = np.ascontiguousarray(ss.reshape(L, 2, 512, 128).transpose(0, 1, 3, 2))
    sc = np.asarray(inp["state_conv"])[layers][:, sample_bs]
    d["cv0"] = np.ascontiguousarray(sc.reshape(L, 2, 3, 8, 128).transpose(0, 1, 4, 3, 2))
    sp = np.asarray(inp["state_pool"])[layers][:, sample_bs]
    d["pl0"] = np.ascontiguousarray(sp.reshape(L, 2, 15, 4, 128).transpose(0, 1, 4, 3, 2))
    return d


def unpack_core(res, L, role, n_ptiles):
    sn = role
    o = {}
    ntok = n_ptiles * TT
    off = 0 if role == 0 else LAG * TT
    o["y"] = res["yT"][:, off:off + ntok].T
    o["ys"] = res["ysT"][sn].T.reshape(2, 64, 1024)
    o["ssm_p"] = res["ssm_p"][sn].transpose(0, 2, 1).reshape(L, 8, 64, 128)
    o["conv_p"] = res["conv_p"][sn].transpose(0, 3, 2, 1).reshape(L, 3, 1024)
    o["pool_p"] = res["pool_p"][sn].transpose(0, 3, 2, 1).reshape(L, 15, 512)
    o["ssm_s"] = res["ssm_s"][sn].transpose(0, 1, 3, 2).reshape(L, 2, 8, 64, 128)
    o["conv_s"] = res["conv_s"][sn].transpose(0, 1, 4, 3, 2).reshape(L, 2, 3, 1024)
    o["pool_s"] = res["pool_s"][sn].transpose(0, 1, 4, 3, 2).reshape(L, 2, 15, 512)
    return o


_NC_CACHE = {}


def kernel(**inp):
    LL = DEPTH // 2
    n_ptiles = 8192 // TT
    key = (n_ptiles, LL)
    if key not in _NC_CACHE:
        _NC_CACHE[key] = build_program(n_ptiles, LL)
    nc = _NC_CACHE[key]
    stage_layers = [list(range(0, LL)), list(range(LL, DEPTH))]
    shared = []
    for role in range(2):
        shared.append({"wsl": make_slabs(inp, stage_layers[role]), "prm": make_params(inp, stage_layers[role], role),
                       "cst": make_consts(role)})
    in_maps = []
    for c in range(8):
        b, role = c // 2, c % 2
        in_maps.append(prep_core_inputs(inp, stage_layers[role], role, b, n_ptiles, [2 * b, 2 * b + 1], shared[role]))
    res = run_bass_kernel_spmd(nc, in_maps, core_ids=list(range(8)))
    outs = [[unpack_core(res.results[2 * b + role], LL, role, n_ptiles) for role in range(2)] for b in range(4)]
    cat_l = lambda k, b: np.concatenate([outs[b][0][k], outs[b][1][k]], 0)
    y_prompt = np.stack([outs[b][1]["y"] for b in range(4)], 0).astype(np.float32)
    y_sample = np.concatenate([outs[b][1]["ys"] for b in range(4)], 0).astype(np.float32)
    ssm_p = np.stack([cat_l("ssm_p", b) for b in range(4)], 1).astype(np.float32)
    conv_p = np.stack([cat_l("conv_p", b) for b in range(4)], 1).astype(np.float32)
    pool_p = np.stack([cat_l("pool_p", b) for b in range(4)], 1).astype(np.float32)
    ssm_s = np.concatenate([cat_l("ssm_s", b) for b in range(4)], 1).astype(np.float32)
    conv_s = np.concatenate([cat_l("conv_s", b) for b in range(4)], 1).astype(np.float32)
    pool_s = np.concatenate([cat_l("pool_s", b) for b in range(4)], 1).astype(np.float32)
    return (np.ascontiguousarray(y_prompt), np.ascontiguousarray(y_sample), np.ascontiguousarray(ssm_p),
            np.ascontiguousarray(conv_p), np.ascontiguousarray(pool_p), np.ascontiguousarray(ssm_s),
            np.ascontiguousarray(conv_s), np.ascontiguousarray(pool_s))
```
